# Optimizing a Trainium2 kernel written in Bass

```python
import jax, jax.numpy as jnp
from jax import lax
import numpy as np

D_MODEL = 1024
BATCH = 8
SEQ = 8192
DEPTH = 1

CHUNK = 64
RET_HEADS = 4
RET_QK_DIM = 128
RET_V_DIM = D_MODEL // RET_HEADS
RET_QK_WIDTH = RET_HEADS * RET_QK_DIM
RET_V_WIDTH = RET_HEADS * RET_V_DIM
LRU_WIDTH = D_MODEL
LRU_BLOCKS = 8
LRU_BLOCK_DIM = LRU_WIDTH // LRU_BLOCKS
CONV_WIDTH = 4
RG_LRU_C = 8.0
ROPE_BASE = 10000.0
LN_EPS = 1e-5
DEEPNORM_ALPHA = (2.0 * DEPTH) ** 0.25
DEEPNORM_BETA = (8.0 * DEPTH) ** -0.25
IN_SPLITS = (RET_QK_WIDTH, RET_QK_WIDTH, RET_V_WIDTH, RET_V_WIDTH, LRU_WIDTH, LRU_WIDTH, D_MODEL, D_MODEL)
IN_WIDTH = sum(IN_SPLITS)

kernel_name = "hybrid_retention_rglru_gated_merge"


def layer_norm(x, g, b):
    xf = x.astype(jnp.float32)
    mu = jnp.mean(xf, axis=-1, keepdims=True)
    var = jnp.mean(jnp.square(xf - mu), axis=-1, keepdims=True)
    y = (xf - mu) * lax.rsqrt(var + LN_EPS)
    return (y * g.astype(jnp.float32) + b.astype(jnp.float32)).astype(x.dtype)


def rotary(x):
    s, d = x.shape[1], x.shape[-1]
    half = d // 2
    inv = ROPE_BASE ** (-2.0 * jnp.arange(half, dtype=jnp.float32) / d)
    ang = jnp.arange(s, dtype=jnp.float32)[:, None] * inv[None, :]
    cos = jnp.cos(ang)[None, :, None, :]
    sin = jnp.sin(ang)[None, :, None, :]
    xf = x.astype(jnp.float32)
    x1, x2 = xf[..., :half], xf[..., half:]
    return jnp.concatenate([x1 * cos - x2 * sin, x2 * cos + x1 * sin], axis=-1)


def chunk_retention(q, k, v):
    b, s, h, dk = q.shape
    dv = v.shape[-1]
    nc = s // CHUNK
    log_g = jnp.log1p(-jnp.exp2(-5.0 - jnp.arange(h, dtype=jnp.float32)))
    qc = q.reshape(b, nc, CHUNK, h, dk)
    kc = (k * (dk ** -0.5)).reshape(b, nc, CHUNK, h, dk)
    vc = v.astype(jnp.float32).reshape(b, nc, CHUNK, h, dv)
    pos = jnp.arange(CHUNK, dtype=jnp.float32)
    intra_decay = jnp.exp(log_g[:, None, None] * jnp.abs(pos[:, None] - pos[None, :]))
    scores = jnp.einsum('bnihd,bnjhd->bnhij', qc, kc) * intra_decay[None, None]
    o_intra = jnp.einsum('bnhij,bnjhe->bnihe', scores, vc)
    k_decay = jnp.exp((CHUNK - 1 - pos)[:, None] * log_g[None, :])
    kv = jnp.einsum('bnjhd,bnjhe,jh->nbhde', kc, vc, k_decay)
    chunk_decay = jnp.exp(log_g * CHUNK)[None, :, None, None]

    def step(state, kv_c):
        return state * chunk_decay + kv_c, state

    _, s_prev = lax.scan(step, jnp.zeros((b, h, dk, dv), jnp.float32), kv)
    q_decay = jnp.exp((pos + 1.0)[:, None] * log_g[None, :])
    o_inter = jnp.einsum('bnihd,nbhde,ih->bnihe', qc, s_prev, q_decay)
    return (o_intra + o_inter).reshape(b, s, h, dv)


def head_group_norm(o, g, bias):
    b, s, h, dv = o.shape
    mu = jnp.mean(o, axis=-1, keepdims=True)
    var = jnp.mean(jnp.square(o - mu), axis=-1, keepdims=True)
    y = ((o - mu) * lax.rsqrt(var + LN_EPS)).reshape(b, s, h * dv)
    return y * g.astype(jnp.float32) + bias.astype(jnp.float32)


def causal_depthwise_conv(x, w, bias):
    s = x.shape[1]
    xp = jnp.pad(x, ((0, 0), (CONV_WIDTH - 1, 0), (0, 0)))
    return bias + sum(xp[:, t:t + s] * w[t] for t in range(CONV_WIDTH))


def rg_lru(x, w_a, b_a, w_x, b_x, lam):
    b, s, w = x.shape
    xb = x.reshape(b, s, LRU_BLOCKS, LRU_BLOCK_DIM)
    r = jax.nn.sigmoid(jnp.einsum('bshi,hij->bshj', xb, w_a).reshape(b, s, w) + b_a)
    i = jax.nn.sigmoid(jnp.einsum('bshi,hij->bshj', xb, w_x).reshape(b, s, w) + b_x)
    log_a = -RG_LRU_C * r.astype(jnp.float32) * jax.nn.softplus(-lam.astype(jnp.float32))
    a = jnp.exp(log_a)
    u = jnp.sqrt(-jnp.expm1(2.0 * log_a)) * (i * x).astype(jnp.float32)

    def combine(left, right):
        a1, b1 = left
        a2, b2 = right
        return a1 * a2, a2 * b1 + b2

    _, hs = lax.associative_scan(combine, (a, u), axis=1)
    return hs.astype(x.dtype)


def setup_inputs(seed: int = 0) -> dict:
    key = jax.random.key(seed)
    ks = jax.random.split(key, 20)
    f32 = jnp.float32
    d = D_MODEL
    x = jax.random.normal(ks[0], (BATCH, SEQ, d), f32)
    ln_in_g = 1.0 + 0.02 * jax.random.normal(ks[1], (d,), f32)
    ln_in_b = 0.02 * jax.random.normal(ks[2], (d,), f32)
    col_scale = np.concatenate([np.full(n, DEEPNORM_BETA if idx == 2 else 1.0, np.float32)
                                for idx, n in enumerate(IN_SPLITS)])
    w_in = jax.random.normal(ks[3], (DEPTH, d, IN_WIDTH), f32) * (d ** -0.5) * jnp.asarray(col_scale)
    b_merge = 0.01 * jax.random.normal(ks[4], (DEPTH, 2, d), f32)
    ret_gn_g = 1.0 + 0.02 * jax.random.normal(ks[5], (DEPTH, RET_V_WIDTH), f32)
    ret_gn_b = 0.02 * jax.random.normal(ks[6], (DEPTH, RET_V_WIDTH), f32)
    w_ret_proj = jax.random.normal(ks[7], (DEPTH, RET_V_WIDTH, d), f32) * (RET_V_WIDTH ** -0.5) * DEEPNORM_BETA
    conv_w = jax.random.normal(ks[8], (DEPTH, CONV_WIDTH, LRU_WIDTH), f32) * (CONV_WIDTH ** -0.5)
    conv_b = 0.01 * jax.random.normal(ks[9], (DEPTH, LRU_WIDTH), f32)
    w_rg_a = jax.random.normal(ks[10], (DEPTH, LRU_BLOCKS, LRU_BLOCK_DIM, LRU_BLOCK_DIM), f32) * (LRU_BLOCK_DIM ** -0.5)
    b_rg_a = 0.01 * jax.random.normal(ks[11], (DEPTH, LRU_WIDTH), f32)
    w_rg_x = jax.random.normal(ks[12], (DEPTH, LRU_BLOCKS, LRU_BLOCK_DIM, LRU_BLOCK_DIM), f32) * (LRU_BLOCK_DIM ** -0.5)
    b_rg_x = 0.01 * jax.random.normal(ks[13], (DEPTH, LRU_WIDTH), f32)
    a0 = jax.random.uniform(ks[14], (DEPTH, LRU_WIDTH), f32, minval=0.9, maxval=0.999)
    a_base = a0 ** (1.0 / RG_LRU_C)
    lru_lambda = jnp.log(a_base) - jnp.log1p(-a_base)
    w_lru_proj = jax.random.normal(ks[15], (DEPTH, LRU_WIDTH, d), f32) * (LRU_WIDTH ** -0.5) * DEEPNORM_BETA
    w_out = jax.random.normal(ks[16], (DEPTH, d, d), f32) * (d ** -0.5) * DEEPNORM_BETA
    ln_out_g = 1.0 + 0.02 * jax.random.normal(ks[17], (DEPTH, d), f32)
    ln_out_b = 0.02 * jax.random.normal(ks[18], (DEPTH, d), f32)
    return {"x": x, "ln_in_g": ln_in_g, "ln_in_b": ln_in_b, "w_in": w_in, "b_merge": b_merge,
            "ret_gn_g": ret_gn_g, "ret_gn_b": ret_gn_b, "w_ret_proj": w_ret_proj,
            "conv_w": conv_w, "conv_b": conv_b, "w_rg_a": w_rg_a, "b_rg_a": b_rg_a,
            "w_rg_x": w_rg_x, "b_rg_x": b_rg_x, "lru_lambda": lru_lambda, "w_lru_proj": w_lru_proj,
            "w_out": w_out, "ln_out_g": ln_out_g, "ln_out_b": ln_out_b}


def reference(x, ln_in_g, ln_in_b, w_in, b_merge, ret_gn_g, ret_gn_b, w_ret_proj, conv_w, conv_b,
              w_rg_a, b_rg_a, w_rg_x, b_rg_x, lru_lambda, w_lru_proj, w_out, ln_out_g, ln_out_b):
    b, s, d = x.shape
    offsets = [int(c) for c in np.cumsum(IN_SPLITS)[:-1]]
    h = layer_norm(x, ln_in_g, ln_in_b)
    for layer in range(DEPTH):
        proj = h @ w_in[layer]
        q, k, v, g_ret, x_lru, g_lru, m_ret, m_lru = jnp.split(proj, offsets, axis=-1)
        qr = rotary(q.reshape(b, s, RET_HEADS, RET_QK_DIM))
        kr = rotary(k.reshape(b, s, RET_HEADS, RET_QK_DIM))
        ret = chunk_retention(qr, kr, v.reshape(b, s, RET_HEADS, RET_V_DIM))
        ret = head_group_norm(ret, ret_gn_g[layer], ret_gn_b[layer]).astype(h.dtype)
        ret_branch = (jax.nn.silu(g_ret) * ret) @ w_ret_proj[layer]
        xc = causal_depthwise_conv(x_lru, conv_w[layer], conv_b[layer])
        hl = rg_lru(xc, w_rg_a[layer], b_rg_a[layer], w_rg_x[layer], b_rg_x[layer], lru_lambda[layer])
        lru_branch = (jax.nn.silu(g_lru) * hl) @ w_lru_proj[layer]
        merged = (jax.nn.sigmoid(m_ret + b_merge[layer, 0]) * ret_branch
                  + jax.nn.sigmoid(m_lru + b_merge[layer, 1]) * lru_branch)
        y = merged @ w_out[layer]
        h = layer_norm(DEEPNORM_ALPHA * h + y, ln_out_g[layer], ln_out_b[layer])
    return h
```

```python
import contextlib
import numpy as np
import concourse.bass as bass
import concourse.mybir as mybir
from concourse.bass_utils import run_bass_kernel_spmd

F32 = mybir.dt.float32
BF16 = mybir.dt.bfloat16
AF = mybir.ActivationFunctionType
ALU = mybir.AluOpType

D = 1024
SEQ = 8192
NCORES = 8
TM = 512
NM = SEQ // TM
NS = TM // 128
NU = 20
RING = 4
LN_EPS = 1e-5
ALPHA = 2.0 ** 0.25
BETA = 8.0 ** -0.25
ENGS = ("pe", "act", "dve", "pool", "sp")

V_CW0, V_CW1, V_CW2, V_CW3, V_CB, V_BA, V_BX, V_LAM, V_BM0, V_BM1, V_GNG, V_GNB = range(12)
NV = 12
Q_HBA, Q_HBX, Q_HBM0, Q_HBM1, Q_NHSP, Q_QSP, Q_HGNG, Q_HGNB, Q_T0, Q_T1 = range(10)
NQ = 10


class Sched:
    def __init__(self, nc, stack, dry=False):
        self.nc = nc
        self.stack = stack
        self.dry = dry
        self.sem = {}
        if not dry:
            for e in ENGS:
                self.sem["E_" + e] = stack.enter_context(nc.semaphore("E_" + e))
        self.cnt = {e: 0 for e in ENGS}
        self.dcnt = {}
        self.waited = {e: {} for e in ENGS}
        self.lastw = {}
        self.readers = {}
        self.ops = {e: [] for e in ENGS}
        self.eng_free = {e: 0.0 for e in ENGS}
        self.fin = {}
        self.step_fin = 0.0
        self.dep_fin = 0.0

    def dsem(self, name):
        if name not in self.dcnt:
            if not self.dry:
                self.sem[name] = self.stack.enter_context(self.nc.semaphore(name))
            self.dcnt[name] = 0
        return name

    def _deps(self, eng, reads, writes):
        deps = {}
        self.dep_fin = 0.0

        def add(rec):
            s, v, src = rec
            f = self.fin.get((s, v), 0.0)
            if f > self.dep_fin:
                self.dep_fin = f
            if src == eng and eng == "pe":
                return
            if deps.get(s, 0) < v:
                deps[s] = v
        for k in reads:
            if k in self.lastw:
                add(self.lastw[k])
        for k in writes:
            if k in self.lastw:
                add(self.lastw[k])
            for s, (v, src) in self.readers.get(k, {}).items():
                add((s, v, src))
        waits = []
        for s, v in deps.items():
            if self.waited[eng].get(s, 0) >= v:
                continue
            self.waited[eng][s] = v
            waits.append((s, v))
        return waits

    def _record(self, rec, reads, writes):
        s, v, src = rec
        for k in reads:
            d = self.readers.setdefault(k, {})
            if d.get(s, (0, None))[0] < v:
                d[s] = (v, src)
        for k in writes:
            prev = self.lastw.get(k)
            assert not (prev is not None and prev[2].startswith("dma:") and not self.readers.get(k)), \
                ("DMA-loaded buffer overwritten before any consumer was emitted", k)
            self.lastw[k] = rec
            self.readers[k] = {}

    COST = {"pe": 0.25, "act": 0.65, "dve": 0.6, "pool": 0.9, "sp": 0.1}

    def op(self, eng, fn, reads=(), writes=(), inc=True, cost=None):
        waits = self._deps(eng, reads, writes)
        if inc:
            self.cnt[eng] += 1
            v = self.cnt[eng]
        else:
            v = self.cnt[eng] + 1
        c = self.COST[eng] if cost is None else cost
        start = max(self.eng_free[eng], self.dep_fin + (0.0 if eng == "pe" else 0.2))
        fin = start + c
        self.eng_free[eng] = fin
        self.fin[("E_" + eng, v)] = max(fin, self.fin.get(("E_" + eng, v), 0.0))
        if fin > self.step_fin:
            self.step_fin = fin
        self._record(("E_" + eng, v, eng), reads, writes)
        self.ops[eng].append((waits, fn, ("E_" + eng, 1) if inc else None))

    def dma(self, q, dsem, out, in_, reads=(), writes=(), **kw):
        self.dsem(dsem)
        waits = self._deps(q, reads, writes)
        self.dcnt[dsem] += 16
        v = self.dcnt[dsem]
        lat = kw.pop("lat", 4.0)
        start = max(self.eng_free[q], self.dep_fin + 0.2)
        self.eng_free[q] = start + 0.1
        self.fin[(dsem, v)] = start + lat
        self._record((dsem, v, "dma:" + dsem), reads, writes)
        self.ops[q].append((waits, lambda e, o=out, i=in_: e.dma_start(out=o, in_=i, **kw), (dsem, 16)))

    def wait_all(self, eng, keys):
        waits = self._deps(eng, (), keys)
        self.ops[eng].append((waits, None, None))

    def emit(self, block):
        me = self

        def run(eng_name, e):
            for waits, fn, inc in me.ops[eng_name]:
                for s, v in waits:
                    e.wait_ge(me.sem[s], v)
                if fn is None:
                    continue
                inst = fn(e)
                if inc is not None:
                    inst.then_inc(me.sem[inc[0]], inc[1])

        @block.tensor
        def _(e):
            run("pe", e)

        @block.scalar
        def _(e):
            run("act", e)

        @block.vector
        def _(e):
            run("dve", e)

        @block.gpsimd
        def _(e):
            run("pool", e)

        @block.sync
        def _(e):
            run("sp", e)


def f_act(out, in_, func, bias=None, scale=None):
    kw = {}
    if bias is not None:
        kw["bias"] = bias
    if scale is not None:
        kw["scale"] = scale
    return lambda e: e.activation(out, in_, func, **kw)


def f_tt(out, a, b, op):
    return lambda e: e.tensor_tensor(out, a, b, op)


def f_stt(out, a, sc, b, op0, op1):
    return lambda e: e.scalar_tensor_tensor(out, a, sc, b, op0, op1)


def f_ts(out, a, s1, s2, op0, op1=None):
    if op1 is None:
        return lambda e: e.tensor_scalar(out, a, s1, None, op0)
    return lambda e: e.tensor_scalar(out, a, s1, s2, op0, op1)


def f_copy(out, in_):
    return lambda e: e.tensor_copy(out, in_)


def f_acopy(out, in_):
    return lambda e: e.copy(out, in_)


def f_mm(out, lhsT, rhs, start, stop):
    return lambda e: e.matmul(out, lhsT, rhs, start=start, stop=stop)


def f_tr(out, in_, ident):
    return lambda e: e.transpose(out, in_, ident)


def f_memset(out, val):
    return lambda e: e.memset(out, val)


class BankPool:
    def __init__(self, S, banks):
        self.S = S
        self.banks = list(banks)
        self.i = 0

    def get(self):
        for _ in range(len(self.banks)):
            b = self.banks[self.i % len(self.banks)]
            self.i += 1
            key = "pb%d" % b
            if key not in self.S.lastw or self.S.readers.get(key):
                return b
        raise AssertionError(("no PSUM bank with emitted consumers", self.banks))


def interleave(gens_w):
    gens = [[g, w, True] for g, w in gens_w]
    while any(a for _, _, a in gens):
        for ent in gens:
            if not ent[2]:
                continue
            for _ in range(ent[1]):
                try:
                    next(ent[0])
                except StopIteration:
                    ent[2] = False
                    break


def schedule(S, gens):
    ready = [0.0 for _ in gens]
    alive = [True for _ in gens]
    base = max(S.eng_free.values()) if False else 0.0
    while any(alive):
        i = min((k for k in range(len(gens)) if alive[k]), key=lambda k: ready[k])
        S.step_fin = ready[i]
        try:
            next(gens[i])
            ready[i] = S.step_fin
        except StopIteration:
            alive[i] = False


def interleave_bg(gens_w, bg_w):
    gens = [[g, w, True] for g, w in gens_w]
    bgs = [[g, w, True] for g, w in bg_w]
    while any(a for _, _, a in gens):
        for ent in gens + bgs:
            if not ent[2]:
                continue
            for _ in range(ent[1]):
                try:
                    next(ent[0])
                except StopIteration:
                    ent[2] = False
                    break


def run_gen(g):
    for _ in g:
        pass


def lockstep(*gs):
    alive = list(gs)
    while alive:
        nxt = []
        for g in alive:
            try:
                next(g)
                nxt.append(g)
            except StopIteration:
                pass
        alive = nxt
        if alive:
            yield


def build_program(n_macro=NM):
    nc = bass.Bass("TRN2", target_bir_lowering=False)
    x_d = nc.dram_tensor("x", [SEQ, D], F32, kind="ExternalInput").ap()
    wpack_d = nc.dram_tensor("wpack", [NU, 128, 4096], F32, kind="ExternalInput").ap()
    rgw_d = nc.dram_tensor("rgw", [128, 2048], F32, kind="ExternalInput").ap()
    vecs_d = nc.dram_tensor("vecs", [128, NV * 8], F32, kind="ExternalInput").ap()
    lnv_d = nc.dram_tensor("lnv", [4, D], F32, kind="ExternalInput").ap()
    rot_d = nc.dram_tensor("rot", [NM, 128, NS * 128], F32, kind="ExternalInput").ap()
    mask_d = nc.dram_tensor("maskT", [128, 512], F32, kind="ExternalInput").ap()
    small_d = nc.dram_tensor("small", [128, 8], F32, kind="ExternalInput").ap()
    ident_d = nc.dram_tensor("ident", [128, 128], F32, kind="ExternalInput").ap()
    out_d = nc.dram_tensor("out", [SEQ, D], F32, kind="ExternalOutput").ap()
    wscr_d = nc.dram_tensor("wscr", [NU, 128, 4096], BF16, kind="Internal").ap()
    hscr_d = nc.dram_tensor("hscr", [SEQ, D], F32, kind="Internal").ap()

    lg = [float(np.log1p(-2.0 ** (-5.0 - h))) for h in range(4)]
    g128 = [float(np.exp(128.0 * lg[h])) for h in range(4)]

    with contextlib.ExitStack() as st:
        def T(name, shape, dt):
            return st.enter_context(nc.sbuf_tensor(name, shape, dt))

        identf = T("identf", [128, 128], F32)
        identb = T("identb", [128, 128], BF16)
        lnig = T("lnig", [128, D], F32)
        lnib = T("lnib", [128, D], F32)
        lnog = T("lnog", [128, D], F32)
        lnob = T("lnob", [128, D], F32)
        maskT = T("maskT_sb", [128, 512], F32)
        small = T("small_sb", [128, 8], F32)
        vecs = T("vecs_sb", [128, NV, 8], F32)
        dq = T("dq", [128, NQ, 8], F32)
        mhalf = T("mhalf", [128, 8], F32)
        rgw = T("rgw_sb", [128, 2, 8, 128], BF16)
        ring = [T("ring%d" % i, [128, 4096], BF16) for i in range(RING)]
        obuf = [T("obuf%d" % i, [128, D], F32) for i in range(2)]
        xs = obuf
        hA = [T("hA%d" % i, [128, D], F32) for i in range(1)]
        hland = hA
        hT = [T("hT%d" % i, [128, 8, TM], BF16) for i in range(3)]
        rot = T("rot_sb", [128, NS * 128], F32)
        qkr = T("qkr", [128, NS, 1024], BF16)
        kdec = T("kdec", [128, NS, 512], BF16)
        vb = T("vb", [128, NS, 1024], BF16)
        rA = T("rA", [128, 512], F32)
        rB = T("rB", [128, 512], F32)
        qkT = [T("qkT%d" % i, [128, 4, 128], BF16) for i in range(1)]
        sTm = [T("sTm%d" % i, [128, 2, 128], BF16) for i in range(1)]
        S32 = T("S32", [128, 1024], F32)
        Sb = T("Sb", [128, 1024], BF16)
        xhat = [T("xhat%d" % i, [128, NS, 512], F32) for i in range(1)]
        gst = [T("gst%d" % i, [128, 12], F32) for i in range(1)]
        gmv = [T("gmv%d" % i, [128, 2, 2], F32) for i in range(1)]
        grs = [T("grs%d" % i, [128, 2], F32) for i in range(1)]
        gnm = [T("gnm%d" % i, [128, 2], F32) for i in range(1)]
        lst = T("lst", [128, 12], F32)
        lmv = T("lmv", [128, 2], F32)
        lrs = T("lrs", [128, 1], F32)
        lst2 = T("lst2", [128, 12], F32)
        lmv2 = T("lmv2", [128, 2], F32)
        lrs2 = T("lrs2", [128, 1], F32)
        retT = [T("retT%d" % i, [128, 8, TM], BF16) for i in range(2)]
        lruT = [T("lruT%d" % i, [128, 8, TM], BF16) for i in range(2)]
        mrg = T("mrg", [128, 8, TM], BF16)
        tg = rA
        yn = rB
        xl = [T("xl%d" % i, [128, 515], F32) for i in range(2)]
        xc = [T("xc%d" % i, [128, 512], F32) for i in range(2)]
        xcb = [T("xcb%d" % i, [128, 512], BF16) for i in range(2)]
        ta = [T("ta%d" % i, [128, 512], F32) for i in range(2)]
        ti = [T("ti%d" % i, [128, 512], F32) for i in range(2)]
        av = [T("av%d" % i, [128, 512], F32) for i in range(2)]
        tgl = [T("tgl%d" % i, [128, 512], F32) for i in range(2)]
        hist = T("hist", [128, 8, 3], F32)
        carry = T("carry", [128, 8], F32)
        tr_ = [T("tr_%d" % i, [128, 512], F32) for i in range(1)]
        tl_ = [T("tl_%d" % i, [128, 512], F32) for i in range(1)]
        pb = [st.enter_context(nc.psum_tensor("pb%d" % i, [128, 512], F32)) for i in range(8)]

        def PK(i):
            return "pb%d" % i

        def emit_all(S, seq, record):
            um = {"resident": {}, "nissued": 0, "free": list(range(RING))}

            def issue_unit(mu):
                m, u = mu
                slot = um["free"].pop(0)
                um["nissued"] += 1
                um["resident"][mu] = slot
                rk = ("ring", slot)
                if m == 0:
                    S.dma("pool", "wc%d" % slot, ring[slot][:], wpack_d[u], writes=[rk], max_dma_last_dim=4096, lat=12.0)
                if m == 0:
                    S.dma("act", "ws%d" % slot, wscr_d[u], ring[slot][:], reads=[rk], writes=[("wscr", u)])
                else:
                    S.dma("sp", "wr%d" % slot, ring[slot][:], wscr_d[u], reads=[("wscr", u)], writes=[rk], lat=7.0)

            def prefetch():
                if seq is None:
                    return
                while um["nissued"] < len(seq) and um["free"]:
                    issue_unit(seq[um["nissued"]])

            def use(m, u):
                if seq is None:
                    if (m, u) not in um["resident"]:
                        um["resident"][(m, u)] = 0
                        record.append((m, u))
                    return 0
                assert (m, u) in um["resident"], ("unit not resident", m, u)
                return um["resident"][(m, u)]

            def done(m, u):
                if seq is None:
                    return
                um["free"].append(um["resident"].pop((m, u)))
                prefetch()

            def tmw(slot, kc):
                return ring[slot][:, kc * 512:(kc + 1) * 512]

            def slab(slot, sl, kc):
                o = (sl * 8 + kc) * 128
                return ring[slot][:, o:o + 128]

            S.dma("sp", "c_id", identf[:], ident_d, writes=["identf"])
            S.dma("sp", "c_lnig", lnig[:], lnv_d[0:1, :].partition_broadcast(128), writes=["lnig"])
            S.dma("sp", "c_lnib", lnib[:], lnv_d[1:2, :].partition_broadcast(128), writes=["lnib"])
            S.dma("sp", "c_lnog", lnog[:], lnv_d[2:3, :].partition_broadcast(128), writes=["lnog"])
            S.dma("sp", "c_lnob", lnob[:], lnv_d[3:4, :].partition_broadcast(128), writes=["lnob"])
            S.dma("sp", "c_mask", maskT[:], mask_d, writes=["maskT"])
            S.dma("sp", "c_small", small[:], small_d, writes=["small"])
            S.dma("sp", "c_vecs", vecs[:].rearrange("p a b -> p (a b)"), vecs_d, writes=["vecs"])
            rgflat = rgw[:].rearrange("p a b c -> p (a b c)")
            for hf in range(2):
                S.dma("sp", "c_rgw%d" % hf, obuf[hf][:], rgw_d[:, hf * 1024:(hf + 1) * 1024], writes=[("obuf", hf)])
                S.op("dve", f_copy(rgflat[:, hf * 1024:(hf + 1) * 1024], obuf[hf][:]), reads=[("obuf", hf)], writes=["rgw"])
            S.op("dve", f_copy(identb[:], identf[:]), reads=["identf"], writes=["identb"])
            S.op("pool", f_memset(mhalf[:], -0.5), writes=["mhalf"])
            S.op("pool", f_memset(S32[:], 0.0), writes=[("S32", h) for h in range(4)])
            S.op("pool", f_memset(Sb[:], 0.0), writes=[("Sb", 0), ("Sb", 1)])
            S.op("pool", f_memset(hist[:].rearrange("p a b -> p (a b)"), 0.0), writes=[("hist", c) for c in range(8)])
            S.op("pool", f_memset(carry[:], 0.0), writes=[("carry", c) for c in range(8)])
            V = lambda i: vecs[:, i, :]
            Q = lambda i: dq[:, i, :]
            S.op("dve", f_ts(Q(Q_HBA), V(V_BA), 0.5, None, ALU.mult), reads=["vecs"], writes=["dq0"])
            S.op("dve", f_ts(Q(Q_HBX), V(V_BX), 0.5, None, ALU.mult), reads=["vecs"], writes=["dq1"])
            S.op("dve", f_ts(Q(Q_HBM0), V(V_BM0), 0.5, None, ALU.mult), reads=["vecs"], writes=["dq2"])
            S.op("dve", f_ts(Q(Q_HBM1), V(V_BM1), 0.5, None, ALU.mult), reads=["vecs"], writes=["dq3"])
            S.op("dve", f_ts(Q(Q_HGNG), V(V_GNG), 0.5, None, ALU.mult), reads=["vecs"], writes=["dq6"])
            S.op("dve", f_ts(Q(Q_HGNB), V(V_GNB), 0.5, None, ALU.mult), reads=["vecs"], writes=["dq7"])
            S.op("act", f_act(Q(Q_T0), V(V_LAM), AF.Exp, scale=-1.0), reads=["vecs"], writes=["dq8"])
            S.op("dve", f_ts(Q(Q_T1), Q(Q_T0), -0.25, 1.0 / 3.0, ALU.mult, ALU.add), reads=["dq8"], writes=["dq9"])
            S.op("dve", f_tt(Q(Q_T1), Q(Q_T1), Q(Q_T0), ALU.mult), reads=["dq8", "dq9"], writes=["dq9"])
            S.op("dve", f_ts(Q(Q_T1), Q(Q_T1), -0.5, None, ALU.add), reads=["dq9"], writes=["dq9"])
            S.op("dve", f_tt(Q(Q_T1), Q(Q_T1), Q(Q_T0), ALU.mult), reads=["dq8", "dq9"], writes=["dq9"])
            S.op("dve", f_ts(Q(Q_T1), Q(Q_T1), 1.0, None, ALU.add), reads=["dq9"], writes=["dq9"])
            S.op("dve", f_tt(Q(Q_T1), Q(Q_T1), Q(Q_T0), ALU.mult), reads=["dq8", "dq9"], writes=["dq9"])
            S.op("dve", f_ts(Q(Q_NHSP), Q(Q_T1), -4.0, None, ALU.mult), reads=["dq9"], writes=["dq4"])
            S.op("dve", f_ts(Q(Q_QSP), Q(Q_T1), 2.0, None, ALU.mult), reads=["dq9"], writes=["dq5"])

            prefetch()

            def load_x(m, s):
                if m >= n_macro:
                    return
                t0 = m * TM + s * 128
                S.dma("sp", "xl%d" % (s % 2), xs[s % 2][:], x_d[t0:t0 + 128, :], writes=[("obuf", s % 2)])

            def load_h(m, s):
                t0 = m * TM + s * 128
                S.dma("sp", "hl0", hland[0][:], hscr_d[t0:t0 + 128, :],
                      reads=[("hscr", m, s)], writes=[("hA", 0)])


            def hTk(m, s, half):
                return ("hT", m % 3, s, half)

            def hT_all(m, kc):
                return [hTk(m, s, kc // 4) for s in range(NS)]

            def Athread(m, pool):
                hTm = hT[m % 3]
                load_x(m, 0)
                load_x(m, 1)
                for s in range(NS):
                    xk = ("obuf", s % 2)
                    xt = xs[s % 2]
                    hk = ("hA", 0)
                    hb = hA[0]
                    S.op("dve", lambda e, xt=xt: e.bn_stats(lst[:, 0:6], xt[:, 0:512]), reads=[xk], writes=["lst0"])
                    S.op("dve", lambda e, xt=xt: e.bn_stats(lst[:, 6:12], xt[:, 512:1024]), reads=[xk], writes=["lst1"])
                    S.op("dve", lambda e: e.bn_aggr(lmv[:], lst[:]), reads=["lst0", "lst1"], writes=["lmv"])
                    yield
                    S.op("pool", f_ts(lrs[:], lmv[:, 1:2], LN_EPS, None, ALU.add), reads=["lmv"], writes=["lrs"])
                    S.op("pool", f_tt(lrs[:], lrs[:], mhalf[:, 0:1], ALU.pow), reads=["lrs", "mhalf"], writes=["lrs"])
                    yield
                    S.op("dve", f_stt(hb[:], xt[:], lmv[:, 0:1], lnig[:], ALU.subtract, ALU.mult),
                         reads=[xk, "lmv", "lnig"], writes=[hk])
                    S.op("dve", f_stt(hb[:], hb[:], lrs[:], lnib[:], ALU.mult, ALU.add),
                         reads=[hk, "lrs", "lnib"], writes=[hk])
                    if s + 2 < NS:
                        load_x(m, s + 2)
                    t0 = m * TM + s * 128
                    S.dma("pool", "hs0", hscr_d[t0:t0 + 128, :], hb[:], reads=[hk], writes=[("hscr", m, s)])
                    yield
                    for half in range(2):
                        b = pool.get()
                        for j in range(4):
                            kc = half * 4 + j
                            S.op("pe", f_tr(pb[b][:, j * 128:(j + 1) * 128], hb[:, kc * 128:(kc + 1) * 128], identf[:]),
                                 reads=[hk, "identf"], writes=[PK(b)], inc=(j == 3))
                        yield
                        src = pb[b][:].rearrange("p (k t) -> p k t", k=4)
                        dst = hTm[:, half * 4:half * 4 + 4, s * 128:(s + 1) * 128]
                        S.op("act", f_acopy(dst, src), reads=[PK(b)], writes=[hTk(m, s, half)])
                    yield

            def Bthread(m, pool):
                hTm = hT[m % 3]
                S.dma("sp", "rotl", rot[:], rot_d[m], writes=["rot"])
                for u in range(4):
                    slot = use(m, u)
                    rk = ("ring", slot)
                    for s in range(NS):
                        b = pool.get()
                        for kc in range(8):
                            S.op("pe", f_mm(pb[b][:], hTm[:, kc, s * 128:(s + 1) * 128], tmw(slot, kc), kc == 0, kc == 7),
                                 reads=[hTk(m, s, kc // 4), rk], writes=[PK(b)], inc=(kc == 7))
                        if s == NS - 1:
                            done(m, u)
                        yield
                        if u < 2:
                            ps4 = pb[b][:].rearrange("p (h two d) -> p h two d", h=4, two=2)
                            rB4 = rB[:].rearrange("p (h two d) -> p h two d", h=4, two=2)
                            rA4 = rA[:].rearrange("p (h two d) -> p h two d", h=4, two=2)
                            cosb = rot[:, s * 128:s * 128 + 64].unsqueeze(1).unsqueeze(1).to_broadcast([128, 4, 2, 64])
                            sinb = rot[:, s * 128 + 64:s * 128 + 128].unsqueeze(1).to_broadcast([128, 4, 64])
                            S.op("dve", f_tt(rA4, ps4, cosb, ALU.mult), reads=[PK(b), "rot"], writes=["rA"])
                            S.op("dve", f_tt(rB4[:, :, 0, :], ps4[:, :, 1, :], sinb, ALU.mult), reads=[PK(b), "rot"], writes=["rB0"])
                            S.op("dve", f_tt(rB4[:, :, 1, :], ps4[:, :, 0, :], sinb, ALU.mult), reads=[PK(b), "rot"], writes=["rB1"])
                            yield
                            if u == 0:
                                q4 = qkr[:, s, 0:512].rearrange("p (h two d) -> p h two d", h=4, two=2)
                                S.op("pool", f_tt(q4[:, :, 0, :], rA4[:, :, 0, :], rB4[:, :, 0, :], ALU.subtract), reads=["rA", "rB0"], writes=[("qr", s)])
                                S.op("pool", f_tt(q4[:, :, 1, :], rA4[:, :, 1, :], rB4[:, :, 1, :], ALU.add), reads=["rA", "rB1"], writes=[("qr", s)])
                            else:
                                S.op("pool", f_tt(rA4[:, :, 0, :], rA4[:, :, 0, :], rB4[:, :, 0, :], ALU.subtract), reads=["rA", "rB0"], writes=["rA"])
                                S.op("pool", f_tt(rA4[:, :, 1, :], rA4[:, :, 1, :], rB4[:, :, 1, :], ALU.add), reads=["rA", "rB1"], writes=["rA"])
                                yield
                                S.op("act", f_acopy(qkr[:, s, 512:1024], rA[:]), reads=["rA"], writes=[("kr", s)])
                                decb = small[:, 0:4].unsqueeze(2).to_broadcast([128, 4, 128])
                                S.op("pool", f_tt(kdec[:, s, :].rearrange("p (h d) -> p h d", h=4),
                                                  rA[:].rearrange("p (h d) -> p h d", h=4), decb, ALU.mult),
                                     reads=["rA", "small"], writes=[("kdec", s)])
                        else:
                            vh = u - 2
                            S.op("act", f_acopy(vb[:, s, vh * 512:(vh + 1) * 512], pb[b][:]), reads=[PK(b)], writes=[("vb", s, vh)])
                        yield

            def Cthread(m, pool):
                for hp in range(2):
                    yield from Cpass(m, hp, pool)

            def Cpass(m, hp, pool):
                hTm = hT[m % 3]
                retTm = retT[m % 2]
                qT_, sT_, gs_, gm_, gr_, gn_, xh_ = qkT[0], sTm[0], gst[0], gmv[0], grs[0], gnm[0], xhat[0]
                K = lambda n: (n, 0)
                tgb, ynb, tgk, ynk = rA, rB, ["rA"], ["rB0", "rB1"]
                for s in range(NS):
                    bT = pool.get()
                    pT = pb[bT][:].bitcast(BF16)
                    for j in range(4):
                        hh = hp * 2 + (j % 2)
                        col = (0 if j < 2 else 512) + hh * 128
                        S.op("pe", f_tr(pT[:, j * 128:(j + 1) * 128], qkr[:, s, col:col + 128], identb[:]),
                             reads=[("qr", s) if j < 2 else ("kr", s), "identb"], writes=[PK(bT)], inc=(j == 3))
                    yield
                    S.op("act", f_acopy(qT_[:].rearrange("p a b -> p (a b)"), pT[:, 0:512]), reads=[PK(bT)], writes=[K("qkT")])
                    yield
                    bS = pool.get()
                    for hl in range(2):
                        S.op("pe", f_mm(pb[bS][:, hl * 128:(hl + 1) * 128], qT_[:, 2 + hl, :], qT_[:, hl, :], True, True),
                             reads=[K("qkT")], writes=[PK(bS)], inc=(hl == 1))
                    bK = pool.get()
                    for hl in range(2):
                        hh = hp * 2 + hl
                        S.op("pe", f_mm(pb[bK][:, hl * 256:(hl + 1) * 256], kdec[:, s, hh * 128:(hh + 1) * 128],
                                        vb[:, s, hh * 256:(hh + 1) * 256], True, True),
                             reads=[("kdec", s), ("vb", s, hp)], writes=[PK(bK)], inc=(hl == 1))
                    yield
                    S.op("dve", f_tt(sT_[:].rearrange("p a b -> p (a b)"), pb[bS][:, 0:256],
                                     maskT[:, hp * 256:(hp + 1) * 256], ALU.mult),
                         reads=[PK(bS), "maskT"], writes=[K("sTm")])
                    yield
                    bO = pool.get()
                    for hl in range(2):
                        hh = hp * 2 + hl
                        S.op("pe", f_mm(pb[bO][:, hl * 256:(hl + 1) * 256], sT_[:, hl, :], vb[:, s, hh * 256:(hh + 1) * 256], True, False),
                             reads=[K("sTm"), ("vb", s, hp)], writes=[PK(bO)], inc=False)
                        S.op("pe", f_mm(pb[bO][:, hl * 256:(hl + 1) * 256], qT_[:, hl, :], Sb[:, hh * 256:(hh + 1) * 256], False, True),
                             reads=[K("qkT"), ("Sb", hp)], writes=[PK(bO)], inc=(hl == 1))
                    yield
                    for hl in range(2):
                        hh = hp * 2 + hl
                        S.op("dve", f_stt(S32[:, hh * 256:(hh + 1) * 256], S32[:, hh * 256:(hh + 1) * 256], g128[hh],
                                          pb[bK][:, hl * 256:(hl + 1) * 256], ALU.mult, ALU.add),
                             reads=[PK(bK), ("S32", hh)], writes=[("S32", hh)])
                    yield
                    S.op("dve", f_copy(Sb[:, hp * 512:(hp + 1) * 512], S32[:, hp * 512:(hp + 1) * 512]),
                         reads=[("S32", hp * 2), ("S32", hp * 2 + 1)], writes=[("Sb", hp)])
                    for hl in range(2):
                        S.op("dve", lambda e, hl=hl, bO=bO: e.bn_stats(gs_[:, hl * 6:(hl + 1) * 6], pb[bO][:, hl * 256:(hl + 1) * 256]),
                             reads=[PK(bO)], writes=[K(("gst", hl))])
                        S.op("dve", lambda e, hl=hl: e.bn_aggr(gm_[:, hl, :], gs_[:, hl * 6:(hl + 1) * 6]),
                             reads=[K(("gst", hl))], writes=[K(("gmv", hl))])
                    yield
                    S.op("pool", f_tt(gr_[:], gm_[:, :, 1], small[:, 4 + hp * 2:6 + hp * 2], ALU.add),
                         reads=[K(("gmv", 0)), K(("gmv", 1)), "small"], writes=[K("grs")])
                    S.op("pool", f_tt(gr_[:], gr_[:], mhalf[:, 0:2], ALU.pow), reads=[K("grs"), "mhalf"], writes=[K("grs")])
                    S.op("pool", f_tt(gn_[:], gm_[:, :, 0], gr_[:], ALU.mult), reads=[K(("gmv", 0)), K(("gmv", 1)), K("grs")], writes=[K("gnm")])
                    S.op("pool", f_ts(gn_[:], gn_[:], -1.0, None, ALU.mult), reads=[K("gnm")], writes=[K("gnm")])
                    yield
                    for hl in range(2):
                        S.op("act", f_act(xh_[:, s, hl * 256:(hl + 1) * 256], pb[bO][:, hl * 256:(hl + 1) * 256], AF.Identity,
                                          bias=gn_[:, hl:hl + 1], scale=gr_[:, hl:hl + 1]),
                             reads=[PK(bO), K("grs"), K("gnm")], writes=[("xhat", 0, s, hl)])
                    yield
                for cl in range(4):
                    c = hp * 4 + cl
                    slot = use(m, 4 + hp)
                    rk = ("ring", slot)
                    bG = pool.get()
                    for kc in range(8):
                        S.op("pe", f_mm(pb[bG][:], slab(slot, cl, kc), hTm[:, kc, :], kc == 0, kc == 7),
                             reads=hT_all(m, kc) + [rk], writes=[PK(bG)], inc=(kc == 7))
                    if cl == 3:
                        done(m, 4 + hp)
                    bX = pool.get()
                    for s in range(NS):
                        S.op("pe", f_tr(pb[bX][:, s * 128:(s + 1) * 128], xh_[:, s, cl * 128:(cl + 1) * 128], identf[:]),
                             reads=[("xhat", 0, s, cl // 2), "identf"], writes=[PK(bX)], inc=(s == NS - 1))
                    yield
                    S.op("act", f_act(tgb[:], pb[bG][:], AF.Tanh, scale=0.5), reads=[PK(bG)], writes=tgk)
                    S.op("act", f_act(ynb[:], pb[bX][:], AF.Identity, bias=dq[:, Q_HGNB, c:c + 1], scale=dq[:, Q_HGNG, c:c + 1]),
                         reads=[PK(bX), "dq6", "dq7"], writes=ynk)
                    yield
                    S.op("dve", f_stt(tgb[:], tgb[:], 1.0, pb[bG][:], ALU.add, ALU.mult), reads=tgk + [PK(bG)], writes=tgk)
                    yield
                    S.op("pool", f_tt(retTm[:, c, :], ynb[:], tgb[:], ALU.mult), reads=ynk + tgk, writes=[("retT", m % 2, c)])
                    yield

            def Dthread(m, pool):
                hTm = hT[m % 3]
                ucount = {}

                def front(c):
                    u = 6 + c // 2
                    slot = use(m, u)
                    rk = ("ring", slot)
                    sl = (c % 2) * 2
                    di = c % 2
                    bXL = pool.get()
                    for kc in range(8):
                        S.op("pe", f_mm(pb[bXL][:], slab(slot, sl, kc), hTm[:, kc, :], kc == 0, kc == 7),
                             reads=hT_all(m, kc) + [rk], writes=[PK(bXL)], inc=(kc == 7))
                    xlk, xck, xcbk = ("xl", di), ("xc", di), ("xcb", di)
                    S.op("pool", f_copy(xl[di][:, 0:3], hist[:, c, :]), reads=[("hist", c)], writes=[xlk])
                    yield
                    S.op("act", f_acopy(xl[di][:, 3:515], pb[bXL][:]), reads=[PK(bXL)], writes=[xlk])
                    S.op("act", f_act(xc[di][:], pb[bXL][:], AF.Identity, bias=vecs[:, V_CB, c:c + 1], scale=vecs[:, V_CW3, c:c + 1]),
                         reads=[PK(bXL), "vecs"], writes=[xck])
                    bGL = pool.get()
                    for kc in range(8):
                        S.op("pe", f_mm(pb[bGL][:], slab(slot, sl + 1, kc), hTm[:, kc, :], kc == 0, kc == 7),
                             reads=hT_all(m, kc) + [rk], writes=[PK(bGL)], inc=(kc == 7))
                    ucount[u] = ucount.get(u, 0) + 1
                    if ucount[u] == 2:
                        done(m, u)
                    yield
                    S.op("pool", f_copy(hist[:, c, :], xl[di][:, 512:515]), reads=[xlk], writes=[("hist", c)])
                    for t, vi in ((2, V_CW2), (1, V_CW1), (0, V_CW0)):
                        S.op("dve", f_stt(xc[di][:], xl[di][:, t:t + 512], vecs[:, vi, c:c + 1], xc[di][:], ALU.mult, ALU.add),
                             reads=[xlk, xck, "vecs"], writes=[xck])
                    S.op("act", f_act(tgl[di][:], pb[bGL][:], AF.Tanh, scale=0.5), reads=[PK(bGL)], writes=[("tgl", di)])
                    yield
                    S.op("dve", f_copy(xcb[di][:], xc[di][:]), reads=[xck], writes=[xcbk])
                    S.op("dve", f_stt(tgl[di][:], tgl[di][:], 1.0, pb[bGL][:], ALU.add, ALU.mult), reads=[("tgl", di), PK(bGL)], writes=[("tgl", di)])
                    yield
                    bA = pool.get()
                    S.op("pe", f_mm(pb[bA][:], rgw[:, 0, c, :], xcb[di][:], True, True), reads=["rgw", xcbk], writes=[PK(bA)])
                    bI = pool.get()
                    S.op("pe", f_mm(pb[bI][:], rgw[:, 1, c, :], xcb[di][:], True, True), reads=["rgw", xcbk], writes=[PK(bI)])
                    S.op("act", f_act(ta[di][:], pb[bA][:], AF.Tanh, bias=dq[:, Q_HBA, c:c + 1], scale=0.5), reads=[PK(bA), "dq0"], writes=[("ta", di)])
                    S.op("act", f_act(ti[di][:], pb[bI][:], AF.Tanh, bias=dq[:, Q_HBX, c:c + 1], scale=0.5), reads=[PK(bI), "dq1"], writes=[("ti", di)])
                    yield
                    S.op("act", f_act(av[di][:], ta[di][:], AF.Exp, bias=dq[:, Q_NHSP, c:c + 1], scale=dq[:, Q_NHSP, c:c + 1]),
                         reads=[("ta", di), "dq4"], writes=[("av", di)])
                    S.op("act", f_act(ta[di][:], ta[di][:], AF.Tanh, bias=dq[:, Q_QSP, c:c + 1], scale=dq[:, Q_QSP, c:c + 1]),
                         reads=[("ta", di), "dq5"], writes=[("ta", di)])
                    S.op("dve", f_stt(ti[di][:], ti[di][:], 1.0, xc[di][:], ALU.add, ALU.mult), reads=[("ti", di), xck], writes=[("ti", di)])
                    yield

                def back(c):
                    di = c % 2
                    S.op("act", f_act(ta[di][:], ta[di][:], AF.Sqrt, scale=1.0 / 16.0), reads=[("ta", di)], writes=[("ta", di)])
                    yield
                    S.op("dve", f_stt(ta[di][:], av[di][:], 1.0, ta[di][:], ALU.add, ALU.mult), reads=[("av", di), ("ta", di)], writes=[("ta", di)])
                    yield
                    S.op("dve", f_tt(ta[di][:], ta[di][:], ti[di][:], ALU.mult), reads=[("ta", di), ("ti", di)], writes=[("ta", di)])
                    yield
                    S.op("dve", lambda e, c=c, di=di: e.tensor_tensor_scan(ti[di][:], av[di][:], ta[di][:], carry[:, c:c + 1], ALU.mult, ALU.add),
                         reads=[("av", di), ("ta", di), ("carry", c)], writes=[("ti", di)])
                    yield
                    S.op("pool", f_copy(carry[:, c:c + 1], ti[di][:, 511:512]), reads=[("ti", di)], writes=[("carry", c)])
                    S.op("pool", f_tt(lruT[m % 2][:, c, :], ti[di][:], tgl[di][:], ALU.mult), reads=[("ti", di), ("tgl", di)], writes=[("lruT", m % 2, c)])
                    yield

                for c0 in range(0, 8, 2):
                    yield from lockstep(front(c0), front(c0 + 1))
                    yield
                    yield from lockstep(back(c0), back(c0 + 1))
                    yield

            def Ethread(m, pool):
                hTm = hT[m % 3]
                for c in range(8):
                    u = 10 + c
                    slot = use(m, u)
                    rk = ("ring", slot)
                    ei = 0
                    bMR = pool.get()
                    for kc in range(8):
                        S.op("pe", f_mm(pb[bMR][:], slab(slot, 0, kc), hTm[:, kc, :], kc == 0, kc == 7),
                             reads=hT_all(m, kc) + [rk], writes=[PK(bMR)], inc=(kc == 7))
                        if kc == 3:
                            yield
                    yield
                    bML = pool.get()
                    for kc in range(8):
                        S.op("pe", f_mm(pb[bML][:], slab(slot, 1, kc), hTm[:, kc, :], kc == 0, kc == 7),
                             reads=hT_all(m, kc) + [rk], writes=[PK(bML)], inc=(kc == 7))
                        if kc == 3:
                            yield
                    S.op("act", f_act(tr_[ei][:], pb[bMR][:], AF.Tanh, bias=dq[:, Q_HBM0, c:c + 1], scale=0.5), reads=[PK(bMR), "dq2"], writes=[("tr", ei)])
                    yield
                    bPR = pool.get()
                    for kc in range(8):
                        S.op("pe", f_mm(pb[bPR][:], slab(slot, 2, kc), retT[m % 2][:, kc, :], kc == 0, kc == 7),
                             reads=[("retT", m % 2, kc), rk], writes=[PK(bPR)], inc=(kc == 7))
                        if kc == 3:
                            yield
                    S.op("act", f_act(tl_[ei][:], pb[bML][:], AF.Tanh, bias=dq[:, Q_HBM1, c:c + 1], scale=0.5), reads=[PK(bML), "dq3"], writes=[("tl", ei)])
                    yield
                    bPL = pool.get()
                    for kc in range(8):
                        S.op("pe", f_mm(pb[bPL][:], slab(slot, 3, kc), lruT[m % 2][:, kc, :], kc == 0, kc == 7),
                             reads=[("lruT", m % 2, kc), rk], writes=[PK(bPL)], inc=(kc == 7))
                        if kc == 3:
                            yield
                    done(m, u)
                    S.op("dve", f_stt(tr_[ei][:], tr_[ei][:], 1.0, pb[bPR][:], ALU.add, ALU.mult), reads=[("tr", ei), PK(bPR)], writes=[("tr", ei)])
                    yield
                    S.op("dve", f_stt(tl_[ei][:], tl_[ei][:], 1.0, pb[bPL][:], ALU.add, ALU.mult), reads=[("tl", ei), PK(bPL)], writes=[("tl", ei)])
                    yield
                    S.op("pool", f_tt(mrg[:, c, :], tr_[ei][:], tl_[ei][:], ALU.add), reads=[("tr", ei), ("tl", ei)], writes=[("mrg", c)])
                    yield

            def Fthread(m, pool):
                load_h(m, 0)
                for s in range(NS):
                    ob = obuf[s % 2]
                    ok = ("obuf", s % 2)
                    hl = hland[0]
                    hlk = ("hA", 0)
                    bY = [pool.get(), pool.get()]
                    for hf in range(2):
                        slot = use(m, 18 + hf)
                        for kc in range(8):
                            S.op("pe", f_mm(pb[bY[hf]][:], mrg[:, kc, s * 128:(s + 1) * 128], tmw(slot, kc), kc == 0, kc == 7),
                                 reads=[("mrg", kc), ("ring", slot)], writes=[PK(bY[hf])], inc=(kc == 7))
                    if s == NS - 1:
                        done(m, 18)
                        done(m, 19)
                    yield
                    for hf in range(2):
                        S.op("dve", f_stt(ob[:, hf * 512:(hf + 1) * 512], hl[:, hf * 512:(hf + 1) * 512], 2.0 * ALPHA,
                                          pb[bY[hf]][:], ALU.mult, ALU.add),
                             reads=[hlk, PK(bY[hf])], writes=[ok])
                    if s + 1 < NS:
                        load_h(m, s + 1)
                    S.op("dve", lambda e, ob=ob: e.bn_stats(lst2[:, 0:6], ob[:, 0:512]), reads=[ok], writes=["lst20"])
                    S.op("dve", lambda e, ob=ob: e.bn_stats(lst2[:, 6:12], ob[:, 512:1024]), reads=[ok], writes=["lst21"])
                    S.op("dve", lambda e: e.bn_aggr(lmv2[:], lst2[:]), reads=["lst20", "lst21"], writes=["lmv2"])
                    yield
                    S.op("pool", f_ts(lrs2[:], lmv2[:, 1:2], 4.0 * LN_EPS, None, ALU.add), reads=["lmv2"], writes=["lrs2"])
                    S.op("pool", f_tt(lrs2[:], lrs2[:], mhalf[:, 0:1], ALU.pow), reads=["lrs2", "mhalf"], writes=["lrs2"])
                    yield
                    S.op("dve", f_stt(ob[:], ob[:], lmv2[:, 0:1], lnog[:], ALU.subtract, ALU.mult), reads=[ok, "lmv2", "lnog"], writes=[ok])
                    S.op("dve", f_stt(ob[:], ob[:], lrs2[:], lnob[:], ALU.mult, ALU.add), reads=[ok, "lrs2", "lnob"], writes=[ok])
                    t0 = m * TM + s * 128
                    S.dma("pool", "os%d" % (s % 2), out_d[t0:t0 + 128, :], ob[:], reads=[ok])
                    yield

            def chain(*gs):
                for g in gs:
                    yield from g

            run_gen(Athread(0, BankPool(S, [6, 7])))
            run_gen(Bthread(0, BankPool(S, [4, 5, 6, 7])))
            th = [(Cthread(0, BankPool(S, [0, 1])), 2), (Dthread(0, BankPool(S, [2, 3, 4])), 1)]
            if n_macro > 1:
                th.append((Athread(1, BankPool(S, [5])), 1))
            interleave(th)
            for m in range(n_macro):
                th = []
                if m + 1 < n_macro:
                    th.append((Bthread(m + 1, BankPool(S, [4, 5, 6, 7])), 2))
                if m >= 1:
                    th.append((Fthread(m - 1, BankPool(S, [0, 1, 2, 3])), 1))
                interleave(th)
                pEA = BankPool(S, [5, 6, 7])
                th = [(Ethread(m, pEA), 2)]
                if m + 1 < n_macro:
                    th.append((Cthread(m + 1, BankPool(S, [0, 1])), 2))
                    th.append((Dthread(m + 1, BankPool(S, [2, 3, 4])), 1))
                if m + 2 < n_macro:
                    th.append((Athread(m + 2, pEA), 1))
                interleave(th)
            run_gen(Fthread(n_macro - 1, BankPool(S, [0, 1, 2, 3])))
            S.wait_all("pool", [("obuf", 0), ("obuf", 1)] + [("hscr", n_macro - 1, s) for s in range(NS)])
            S.wait_all("sp", [("ring", i) for i in range(RING)] + [("hA", 0), "rot"])
            S.wait_all("act", [("ring", i) for i in range(RING)] + [("wscr", u) for u in range(NU)])

        record = []
        emit_all(Sched(nc, None, dry=True), None, record)
        S = Sched(nc, st)
        emit_all(S, record, None)
        with nc.Block() as block:
            S.emit(block)
    return nc


def _tm_unit(W, col0):
    blk = W[:, col0:col0 + 512].reshape(8, 128, 512).transpose(1, 0, 2)
    return np.ascontiguousarray(blk).reshape(128, 4096)


def _slab_unit(slabs):
    out = np.empty((128, 4, 8, 128), np.float32)
    for i, (W, col0) in enumerate(slabs):
        out[:, i] = W[:, col0:col0 + 128].reshape(8, 128, 128).transpose(1, 0, 2)
    return out.reshape(128, 4096)


def _fm(v):
    return np.ascontiguousarray(np.asarray(v, np.float32).reshape(8, 128).T)


def _const_tables():
    half = 64
    inv = (np.float32(10000.0) ** (np.float32(-2.0) * np.arange(half, dtype=np.float32) / np.float32(128))).astype(np.float32)
    ang = (np.arange(SEQ, dtype=np.float32)[:, None] * inv[None, :]).astype(np.float32)
    cos = np.cos(ang.astype(np.float64)).astype(np.float32)
    sin = np.sin(ang.astype(np.float64)).astype(np.float32)
    tab = np.concatenate([cos, sin], axis=1)
    rot = tab.reshape(NM, NS, 128, 128).transpose(0, 2, 1, 3).reshape(NM, 128, NS * 128)
    rot = np.ascontiguousarray(rot)
    lg = np.log1p(-np.exp2(-5.0 - np.arange(4, dtype=np.float64)))
    i = np.arange(128, dtype=np.float64)[:, None]
    j = np.arange(128, dtype=np.float64)[None, :]
    ci = (i // 64)
    cj = (j // 64)
    maskT = np.zeros((128, 4, 128), np.float64)
    for h in range(4):
        e = np.where(ci == cj, np.abs(i - j), i - j) - (i + 1.0)
        mk = np.where(cj > ci, 0.0, np.exp(lg[h] * e)) * (128.0 ** -0.5)
        maskT[:, h, :] = mk.T
    small = np.zeros((128, 8), np.float64)
    jj = np.arange(128, dtype=np.float64)
    for h in range(4):
        small[:, h] = (128.0 ** -0.5) * np.exp(lg[h] * (127.0 - jj))
        small[:, 4 + h] = LN_EPS / np.exp(2.0 * lg[h] * (jj + 1.0))
    return rot, maskT.reshape(128, 512).astype(np.float32), small.astype(np.float32)


def _prep_shared(inp):
    W = np.asarray(inp["w_in"], np.float32)[0]
    Pr = np.asarray(inp["w_ret_proj"], np.float32)[0]
    Pl = np.asarray(inp["w_lru_proj"], np.float32)[0]
    Wo = np.asarray(inp["w_out"], np.float32)[0]
    units = []
    units.append(_tm_unit(W, 0))
    units.append(_tm_unit(W, 512))
    units.append(_tm_unit(W, 1024))
    units.append(_tm_unit(W, 1536))
    for hp in range(2):
        units.append(_slab_unit([(W, 2048 + (hp * 4 + cl) * 128) for cl in range(4)]))
    for jd in range(4):
        c0, c1 = 2 * jd, 2 * jd + 1
        units.append(_slab_unit([(W, 3072 + c0 * 128), (W, 4096 + c0 * 128), (W, 3072 + c1 * 128), (W, 4096 + c1 * 128)]))
    for c in range(8):
        units.append(_slab_unit([(W, 5120 + c * 128), (W, 6144 + c * 128), (Pr, c * 128), (Pl, c * 128)]))
    units.append(_tm_unit(Wo, 0))
    units.append(_tm_unit(Wo, 512))
    wpack = np.stack(units, 0)
    wa = np.asarray(inp["w_rg_a"], np.float32)[0]
    wx = np.asarray(inp["w_rg_x"], np.float32)[0]
    rgw = np.stack([wa.transpose(1, 0, 2), wx.transpose(1, 0, 2)], axis=1)
    rgw = np.ascontiguousarray(rgw).reshape(128, 2048)
    cw = np.asarray(inp["conv_w"], np.float32)[0]
    bm = np.asarray(inp["b_merge"], np.float32)[0]
    vl = [cw[0], cw[1], cw[2], cw[3], inp["conv_b"][0], inp["b_rg_a"][0], inp["b_rg_x"][0], inp["lru_lambda"][0],
          bm[0], bm[1], inp["ret_gn_g"][0], inp["ret_gn_b"][0]]
    vecs = np.stack([_fm(v) for v in vl], axis=1).reshape(128, NV * 8)
    lnv = np.stack([np.asarray(inp["ln_in_g"], np.float32), np.asarray(inp["ln_in_b"], np.float32),
                    np.asarray(inp["ln_out_g"], np.float32)[0], np.asarray(inp["ln_out_b"], np.float32)[0]], 0)
    rot, maskT, small = _const_tables()
    return {"wpack": wpack, "rgw": rgw, "vecs": np.ascontiguousarray(vecs), "lnv": np.ascontiguousarray(lnv),
            "rot": rot, "maskT": maskT, "small": small, "ident": np.eye(128, dtype=np.float32)}


def kernel(**inputs):
    x = np.asarray(inputs["x"], np.float32)
    shared = _prep_shared(inputs)
    nc = build_program()
    in_maps = []
    for b in range(NCORES):
        d = dict(shared)
        d["x"] = np.ascontiguousarray(x[b])
        in_maps.append(d)
    res = run_bass_kernel_spmd(nc, in_maps, core_ids=list(range(NCORES)))
    out = np.stack([np.asarray(res.results[b]["out"], np.float32) for b in range(NCORES)], 0)
    return out
```

```python
import contextlib
import numpy as np
import concourse.bass as bass
import concourse.mybir as mybir
from concourse.bass_utils import run_bass_kernel_spmd

F32 = mybir.dt.float32
BF16 = mybir.dt.bfloat16
AF = mybir.ActivationFunctionType
ALU = mybir.AluOpType

D = 1024
SEQ = 8192
NCORES = 8
TM = 512
NM = SEQ // TM
NS = TM // 128
NU = 20
RING = 4
LN_EPS = 1e-5
ALPHA = 2.0 ** 0.25
BETA = 8.0 ** -0.25
ENGS = ("pe", "act", "dve", "pool", "sp")

V_CW0, V_CW1, V_CW2, V_CW3, V_CB, V_BA, V_BX, V_LAM, V_BM0, V_BM1, V_GNG, V_GNB = range(12)
NV = 12
Q_HBA, Q_HBX, Q_HBM0, Q_HBM1, Q_NHSP, Q_QSP, Q_HGNG, Q_HGNB, Q_T0, Q_T1 = range(10)
NQ = 10


class Sched:
    def __init__(self, nc, stack, dry=False):
        self.nc = nc
        self.stack = stack
        self.dry = dry
        self.sem = {}
        if not dry:
            for e in ENGS:
                self.sem["E_" + e] = stack.enter_context(nc.semaphore("E_" + e))
        self.cnt = {e: 0 for e in ENGS}
        self.dcnt = {}
        self.waited = {e: {} for e in ENGS}
        self.lastw = {}
        self.readers = {}
        self.ops = {e: [] for e in ENGS}
        self.eng_free = {e: 0.0 for e in ENGS}
        self.fin = {}
        self.step_fin = 0.0
        self.dep_fin = 0.0

    def dsem(self, name):
        if name not in self.dcnt:
            if not self.dry:
                self.sem[name] = self.stack.enter_context(self.nc.semaphore(name))
            self.dcnt[name] = 0
        return name

    def _deps(self, eng, reads, writes):
        deps = {}
        self.dep_fin = 0.0

        def add(rec):
            s, v, src = rec
            f = self.fin.get((s, v), 0.0)
            if f > self.dep_fin:
                self.dep_fin = f
            if src == eng and eng == "pe":
                return
            if deps.get(s, 0) < v:
                deps[s] = v
        for k in reads:
            if k in self.lastw:
                add(self.lastw[k])
        for k in writes:
            if k in self.lastw:
                add(self.lastw[k])
            for s, (v, src) in self.readers.get(k, {}).items():
                add((s, v, src))
        waits = []
        for s, v in deps.items():
            if self.waited[eng].get(s, 0) >= v:
                continue
            self.waited[eng][s] = v
            waits.append((s, v))
        return waits

    def _record(self, rec, reads, writes):
        s, v, src = rec
        for k in reads:
            d = self.readers.setdefault(k, {})
            if d.get(s, (0, None))[0] < v:
                d[s] = (v, src)
        for k in writes:
            prev = self.lastw.get(k)
            assert not (prev is not None and prev[2].startswith("dma:") and not self.readers.get(k)), \
                ("DMA-loaded buffer overwritten before any consumer was emitted", k)
            self.lastw[k] = rec
            self.readers[k] = {}

    COST = {"pe": 0.25, "act": 0.65, "dve": 0.6, "pool": 0.9, "sp": 0.1}

    def op(self, eng, fn, reads=(), writes=(), inc=True, cost=None):
        waits = self._deps(eng, reads, writes)
        if inc:
            self.cnt[eng] += 1
            v = self.cnt[eng]
        else:
            v = self.cnt[eng] + 1
        c = self.COST[eng] if cost is None else cost
        start = max(self.eng_free[eng], self.dep_fin + (0.0 if eng == "pe" else 0.2))
        fin = start + c
        self.eng_free[eng] = fin
        self.fin[("E_" + eng, v)] = max(fin, self.fin.get(("E_" + eng, v), 0.0))
        if fin > self.step_fin:
            self.step_fin = fin
        self._record(("E_" + eng, v, eng), reads, writes)
        self.ops[eng].append((waits, fn, ("E_" + eng, 1) if inc else None))

    def dma(self, q, dsem, out, in_, reads=(), writes=(), **kw):
        self.dsem(dsem)
        waits = self._deps(q, reads, writes)
        self.dcnt[dsem] += 16
        v = self.dcnt[dsem]
        lat = kw.pop("lat", 4.0)
        start = max(self.eng_free[q], self.dep_fin + 0.2)
        self.eng_free[q] = start + 0.1
        self.fin[(dsem, v)] = start + lat
        self._record((dsem, v, "dma:" + dsem), reads, writes)
        self.ops[q].append((waits, lambda e, o=out, i=in_: e.dma_start(out=o, in_=i, **kw), (dsem, 16)))

    def wait_all(self, eng, keys):
        waits = self._deps(eng, (), keys)
        self.ops[eng].append((waits, None, None))

    def emit(self, block):
        me = self

        def run(eng_name, e):
            for waits, fn, inc in me.ops[eng_name]:
                for s, v in waits:
                    e.wait_ge(me.sem[s], v)
                if fn is None:
                    continue
                inst = fn(e)
                if inc is not None:
                    inst.then_inc(me.sem[inc[0]], inc[1])

        @block.tensor
        def _(e):
            run("pe", e)

        @block.scalar
        def _(e):
            run("act", e)

        @block.vector
        def _(e):
            run("dve", e)

        @block.gpsimd
        def _(e):
            run("pool", e)

        @block.sync
        def _(e):
            run("sp", e)


def f_act(out, in_, func, bias=None, scale=None):
    kw = {}
    if bias is not None:
        kw["bias"] = bias
    if scale is not None:
        kw["scale"] = scale
    return lambda e: e.activation(out, in_, func, **kw)


def f_tt(out, a, b, op):
    return lambda e: e.tensor_tensor(out, a, b, op)


def f_stt(out, a, sc, b, op0, op1):
    return lambda e: e.scalar_tensor_tensor(out, a, sc, b, op0, op1)


def f_ts(out, a, s1, s2, op0, op1=None):
    if op1 is None:
        return lambda e: e.tensor_scalar(out, a, s1, None, op0)
    return lambda e: e.tensor_scalar(out, a, s1, s2, op0, op1)


def f_copy(out, in_):
    return lambda e: e.tensor_copy(out, in_)


def f_acopy(out, in_):
    return lambda e: e.copy(out, in_)


def f_mm(out, lhsT, rhs, start, stop):
    return lambda e: e.matmul(out, lhsT, rhs, start=start, stop=stop)


def f_tr(out, in_, ident):
    return lambda e: e.transpose(out, in_, ident)


def f_memset(out, val):
    return lambda e: e.memset(out, val)


class BankPool:
    def __init__(self, S, banks):
        self.S = S
        self.banks = list(banks)
        self.i = 0

    def get(self):
        for _ in range(len(self.banks)):
            b = self.banks[self.i % len(self.banks)]
            self.i += 1
            key = "pb%d" % b
            if key not in self.S.lastw or self.S.readers.get(key):
                return b
        raise AssertionError(("no PSUM bank with emitted consumers", self.banks))


def interleave(gens_w):
    gens = [[g, w, True] for g, w in gens_w]
    while any(a for _, _, a in gens):
        for ent in gens:
            if not ent[2]:
                continue
            for _ in range(ent[1]):
                try:
                    next(ent[0])
                except StopIteration:
                    ent[2] = False
                    break


def schedule(S, gens):
    ready = [0.0 for _ in gens]
    alive = [True for _ in gens]
    base = max(S.eng_free.values()) if False else 0.0
    while any(alive):
        i = min((k for k in range(len(gens)) if alive[k]), key=lambda k: ready[k])
        S.step_fin = ready[i]
        try:
            next(gens[i])
            ready[i] = S.step_fin
        except StopIteration:
            alive[i] = False


def interleave_bg(gens_w, bg_w):
    gens = [[g, w, True] for g, w in gens_w]
    bgs = [[g, w, True] for g, w in bg_w]
    while any(a for _, _, a in gens):
        for ent in gens + bgs:
            if not ent[2]:
                continue
            for _ in range(ent[1]):
                try:
                    next(ent[0])
                except StopIteration:
                    ent[2] = False
                    break


def run_gen(g):
    for _ in g:
        pass


def lockstep(*gs):
    alive = list(gs)
    while alive:
        nxt = []
        for g in alive:
            try:
                next(g)
                nxt.append(g)
            except StopIteration:
                pass
        alive = nxt
        if alive:
            yield


def build_program(n_macro=NM):
    nc = bass.Bass("TRN2", target_bir_lowering=False)
    x_d = nc.dram_tensor("x", [SEQ, D], F32, kind="ExternalInput").ap()
    wpack_d = nc.dram_tensor("wpack", [NU, 128, 4096], F32, kind="ExternalInput").ap()
    rgw_d = nc.dram_tensor("rgw", [128, 2048], F32, kind="ExternalInput").ap()
    vecs_d = nc.dram_tensor("vecs", [128, NV * 8], F32, kind="ExternalInput").ap()
    lnv_d = nc.dram_tensor("lnv", [4, D], F32, kind="ExternalInput").ap()
    rot_d = nc.dram_tensor("rot", [NM, 128, NS * 128], F32, kind="ExternalInput").ap()
    mask_d = nc.dram_tensor("maskT", [128, 512], F32, kind="ExternalInput").ap()
    small_d = nc.dram_tensor("small", [128, 8], F32, kind="ExternalInput").ap()
    ident_d = nc.dram_tensor("ident", [128, 128], F32, kind="ExternalInput").ap()
    out_d = nc.dram_tensor("out", [SEQ, D], F32, kind="ExternalOutput").ap()
    wscr_d = nc.dram_tensor("wscr", [NU, 128, 4096], BF16, kind="Internal").ap()
    hscr_d = nc.dram_tensor("hscr", [SEQ, D], F32, kind="Internal").ap()

    lg = [float(np.log1p(-2.0 ** (-5.0 - h))) for h in range(4)]
    g128 = [float(np.exp(128.0 * lg[h])) for h in range(4)]

    with contextlib.ExitStack() as st:
        def T(name, shape, dt):
            return st.enter_context(nc.sbuf_tensor(name, shape, dt))

        identf = T("identf", [128, 128], F32)
        identb = T("identb", [128, 128], BF16)
        lnig = T("lnig", [128, D], F32)
        lnib = T("lnib", [128, D], F32)
        lnog = T("lnog", [128, D], F32)
        lnob = T("lnob", [128, D], F32)
        maskT = T("maskT_sb", [128, 512], F32)
        small = T("small_sb", [128, 8], F32)
        vecs = T("vecs_sb", [128, NV, 8], F32)
        dq = T("dq", [128, NQ, 8], F32)
        mhalf = T("mhalf", [128, 8], F32)
        rgw = T("rgw_sb", [128, 2, 8, 128], BF16)
        ring = [T("ring%d" % i, [128, 4096], BF16) for i in range(RING)]
        obuf = [T("obuf%d" % i, [128, D], F32) for i in range(2)]
        xs = obuf
        hA = [T("hA%d" % i, [128, D], F32) for i in range(1)]
        hland = hA
        hT = [T("hT%d" % i, [128, 8, TM], BF16) for i in range(3)]
        rot = T("rot_sb", [128, NS * 128], F32)
        qkr = T("qkr", [128, NS, 1024], BF16)
        kdec = T("kdec", [128, NS, 512], BF16)
        vb = T("vb", [128, NS, 1024], BF16)
        rA = T("rA", [128, 512], F32)
        rB = T("rB", [128, 512], F32)
        qkT = [T("qkT%d" % i, [128, 4, 128], BF16) for i in range(1)]
        sTm = [T("sTm%d" % i, [128, 2, 128], BF16) for i in range(1)]
        S32 = T("S32", [128, 1024], F32)
        Sb = T("Sb", [128, 1024], BF16)
        xhat = [T("xhat%d" % i, [128, NS, 512], F32) for i in range(1)]
        gst = [T("gst%d" % i, [128, 12], F32) for i in range(1)]
        gmv = [T("gmv%d" % i, [128, 2, 2], F32) for i in range(1)]
        grs = [T("grs%d" % i, [128, 2], F32) for i in range(1)]
        gnm = [T("gnm%d" % i, [128, 2], F32) for i in range(1)]
        lst = T("lst", [128, 12], F32)
        lmv = T("lmv", [128, 2], F32)
        lrs = T("lrs", [128, 1], F32)
        lst2 = T("lst2", [128, 12], F32)
        lmv2 = T("lmv2", [128, 2], F32)
        lrs2 = T("lrs2", [128, 1], F32)
        retT = [T("retT%d" % i, [128, 8, TM], BF16) for i in range(2)]
        lruT = [T("lruT%d" % i, [128, 8, TM], BF16) for i in range(2)]
        mrg = T("mrg", [128, 8, TM], BF16)
        tg = rA
        yn = rB
        xl = [T("xl%d" % i, [128, 515], F32) for i in range(2)]
        xc = [T("xc%d" % i, [128, 512], F32) for i in range(2)]
        xcb = [T("xcb%d" % i, [128, 512], BF16) for i in range(2)]
        ta = [T("ta%d" % i, [128, 512], F32) for i in range(2)]
        ti = [T("ti%d" % i, [128, 512], F32) for i in range(2)]
        av = [T("av%d" % i, [128, 512], F32) for i in range(2)]
        tgl = [T("tgl%d" % i, [128, 512], F32) for i in range(2)]
        hist = T("hist", [128, 8, 3], F32)
        carry = T("carry", [128, 8], F32)
        tr_ = [T("tr_%d" % i, [128, 512], F32) for i in range(1)]
        tl_ = [T("tl_%d" % i, [128, 512], F32) for i in range(1)]
        pb = [st.enter_context(nc.psum_tensor("pb%d" % i, [128, 512], F32)) for i in range(8)]

        def PK(i):
            return "pb%d" % i

        def emit_all(S, seq, record):
            um = {"resident": {}, "nissued": 0, "free": list(range(RING))}

            def issue_unit(mu):
                m, u = mu
                slot = um["free"].pop(0)
                um["nissued"] += 1
                um["resident"][mu] = slot
                rk = ("ring", slot)
                if m == 0:
                    S.dma("pool", "wc%d" % slot, ring[slot][:], wpack_d[u], writes=[rk], max_dma_last_dim=4096, lat=12.0)
                if m == 0:
                    S.dma("act", "ws%d" % slot, wscr_d[u], ring[slot][:], reads=[rk], writes=[("wscr", u)])
                else:
                    S.dma("sp", "wr%d" % slot, ring[slot][:], wscr_d[u], reads=[("wscr", u)], writes=[rk], lat=7.0)

            def prefetch():
                if seq is None:
                    return
                while um["nissued"] < len(seq) and um["free"]:
                    issue_unit(seq[um["nissued"]])

            def use(m, u):
                if seq is None:
                    if (m, u) not in um["resident"]:
                        um["resident"][(m, u)] = 0
                        record.append((m, u))
                    return 0
                assert (m, u) in um["resident"], ("unit not resident", m, u)
                return um["resident"][(m, u)]

            def done(m, u):
                if seq is None:
                    return
                um["free"].append(um["resident"].pop((m, u)))
                prefetch()

            def tmw(slot, kc):
                return ring[slot][:, kc * 512:(kc + 1) * 512]

            def slab(slot, sl, kc):
                o = (sl * 8 + kc) * 128
                return ring[slot][:, o:o + 128]

            S.dma("sp", "c_id", identf[:], ident_d, writes=["identf"])
            S.dma("sp", "c_lnig", lnig[:], lnv_d[0:1, :].partition_broadcast(128), writes=["lnig"])
            S.dma("sp", "c_lnib", lnib[:], lnv_d[1:2, :].partition_broadcast(128), writes=["lnib"])
            S.dma("sp", "c_lnog", lnog[:], lnv_d[2:3, :].partition_broadcast(128), writes=["lnog"])
            S.dma("sp", "c_lnob", lnob[:], lnv_d[3:4, :].partition_broadcast(128), writes=["lnob"])
            S.dma("sp", "c_mask", maskT[:], mask_d, writes=["maskT"])
            S.dma("sp", "c_small", small[:], small_d, writes=["small"])
            S.dma("sp", "c_vecs", vecs[:].rearrange("p a b -> p (a b)"), vecs_d, writes=["vecs"])
            rgflat = rgw[:].rearrange("p a b c -> p (a b c)")
            for hf in range(2):
                S.dma("sp", "c_rgw%d" % hf, obuf[hf][:], rgw_d[:, hf * 1024:(hf + 1) * 1024], writes=[("obuf", hf)])
                S.op("dve", f_copy(rgflat[:, hf * 1024:(hf + 1) * 1024], obuf[hf][:]), reads=[("obuf", hf)], writes=["rgw"])
            S.op("dve", f_copy(identb[:], identf[:]), reads=["identf"], writes=["identb"])
            S.op("pool", f_memset(mhalf[:], -0.5), writes=["mhalf"])
            S.op("pool", f_memset(S32[:], 0.0), writes=[("S32", h) for h in range(4)])
            S.op("pool", f_memset(Sb[:], 0.0), writes=[("Sb", 0), ("Sb", 1)])
            S.op("pool", f_memset(hist[:].rearrange("p a b -> p (a b)"), 0.0), writes=[("hist", c) for c in range(8)])
            S.op("pool", f_memset(carry[:], 0.0), writes=[("carry", c) for c in range(8)])
            V = lambda i: vecs[:, i, :]
            Q = lambda i: dq[:, i, :]
            S.op("dve", f_ts(Q(Q_HBA), V(V_BA), 0.5, None, ALU.mult), reads=["vecs"], writes=["dq0"])
            S.op("dve", f_ts(Q(Q_HBX), V(V_BX), 0.5, None, ALU.mult), reads=["vecs"], writes=["dq1"])
            S.op("dve", f_ts(Q(Q_HBM0), V(V_BM0), 0.5, None, ALU.mult), reads=["vecs"], writes=["dq2"])
            S.op("dve", f_ts(Q(Q_HBM1), V(V_BM1), 0.5, None, ALU.mult), reads=["vecs"], writes=["dq3"])
            S.op("dve", f_ts(Q(Q_HGNG), V(V_GNG), 0.5, None, ALU.mult), reads=["vecs"], writes=["dq6"])
            S.op("dve", f_ts(Q(Q_HGNB), V(V_GNB), 0.5, None, ALU.mult), reads=["vecs"], writes=["dq7"])
            S.op("act", f_act(Q(Q_T0), V(V_LAM), AF.Exp, scale=-1.0), reads=["vecs"], writes=["dq8"])
            S.op("dve", f_ts(Q(Q_T1), Q(Q_T0), -0.25, 1.0 / 3.0, ALU.mult, ALU.add), reads=["dq8"], writes=["dq9"])
            S.op("dve", f_tt(Q(Q_T1), Q(Q_T1), Q(Q_T0), ALU.mult), reads=["dq8", "dq9"], writes=["dq9"])
            S.op("dve", f_ts(Q(Q_T1), Q(Q_T1), -0.5, None, ALU.add), reads=["dq9"], writes=["dq9"])
            S.op("dve", f_tt(Q(Q_T1), Q(Q_T1), Q(Q_T0), ALU.mult), reads=["dq8", "dq9"], writes=["dq9"])
            S.op("dve", f_ts(Q(Q_T1), Q(Q_T1), 1.0, None, ALU.add), reads=["dq9"], writes=["dq9"])
            S.op("dve", f_tt(Q(Q_T1), Q(Q_T1), Q(Q_T0), ALU.mult), reads=["dq8", "dq9"], writes=["dq9"])
            S.op("dve", f_ts(Q(Q_NHSP), Q(Q_T1), -4.0, None, ALU.mult), reads=["dq9"], writes=["dq4"])
            S.op("dve", f_ts(Q(Q_QSP), Q(Q_T1), 2.0, None, ALU.mult), reads=["dq9"], writes=["dq5"])

            prefetch()

            def load_x(m, s):
                if m >= n_macro:
                    return
                t0 = m * TM + s * 128
                S.dma("sp", "xl%d" % (s % 2), xs[s % 2][:], x_d[t0:t0 + 128, :], writes=[("obuf", s % 2)])

            def load_h(m, s):
                t0 = m * TM + s * 128
                S.dma("sp", "hl0", hland[0][:], hscr_d[t0:t0 + 128, :],
                      reads=[("hscr", m, s)], writes=[("hA", 0)])


            def hTk(m, s, half):
                return ("hT", m % 3, s, half)

            def hT_all(m, kc):
                return [hTk(m, s, kc // 4) for s in range(NS)]

            def Athread(m, pool):
                hTm = hT[m % 3]
                load_x(m, 0)
                load_x(m, 1)
                for s in range(NS):
                    xk = ("obuf", s % 2)
                    xt = xs[s % 2]
                    hk = ("hA", 0)
                    hb = hA[0]
                    S.op("dve", lambda e, xt=xt: e.bn_stats(lst[:, 0:6], xt[:, 0:512]), reads=[xk], writes=["lst0"])
                    S.op("dve", lambda e, xt=xt: e.bn_stats(lst[:, 6:12], xt[:, 512:1024]), reads=[xk], writes=["lst1"])
                    S.op("dve", lambda e: e.bn_aggr(lmv[:], lst[:]), reads=["lst0", "lst1"], writes=["lmv"])
                    yield
                    S.op("pool", f_ts(lrs[:], lmv[:, 1:2], LN_EPS, None, ALU.add), reads=["lmv"], writes=["lrs"])
                    S.op("pool", f_tt(lrs[:], lrs[:], mhalf[:, 0:1], ALU.pow), reads=["lrs", "mhalf"], writes=["lrs"])
                    yield
                    S.op("dve", f_stt(hb[:], xt[:], lmv[:, 0:1], lnig[:], ALU.subtract, ALU.mult),
                         reads=[xk, "lmv", "lnig"], writes=[hk])
                    S.op("dve", f_stt(hb[:], hb[:], lrs[:], lnib[:], ALU.mult, ALU.add),
                         reads=[hk, "lrs", "lnib"], writes=[hk])
                    if s + 2 < NS:
                        load_x(m, s + 2)
                    t0 = m * TM + s * 128
                    S.dma("pool", "hs0", hscr_d[t0:t0 + 128, :], hb[:], reads=[hk], writes=[("hscr", m, s)])
                    yield
                    for half in range(2):
                        b = pool.get()
                        for j in range(4):
                            kc = half * 4 + j
                            S.op("pe", f_tr(pb[b][:, j * 128:(j + 1) * 128], hb[:, kc * 128:(kc + 1) * 128], identf[:]),
                                 reads=[hk, "identf"], writes=[PK(b)], inc=(j == 3))
                        yield
                        src = pb[b][:].rearrange("p (k t) -> p k t", k=4)
                        dst = hTm[:, half * 4:half * 4 + 4, s * 128:(s + 1) * 128]
                        S.op("act", f_acopy(dst, src), reads=[PK(b)], writes=[hTk(m, s, half)])
                    yield

            def Bthread(m, pool):
                hTm = hT[m % 3]
                S.dma("sp", "rotl", rot[:], rot_d[m], writes=["rot"])
                for u in range(4):
                    slot = use(m, u)
                    rk = ("ring", slot)
                    for s in range(NS):
                        b = pool.get()
                        for kc in range(8):
                            S.op("pe", f_mm(pb[b][:], hTm[:, kc, s * 128:(s + 1) * 128], tmw(slot, kc), kc == 0, kc == 7),
                                 reads=[hTk(m, s, kc // 4), rk], writes=[PK(b)], inc=(kc == 7))
                        if s == NS - 1:
                            done(m, u)
                        yield
                        if u < 2:
                            ps4 = pb[b][:].rearrange("p (h two d) -> p h two d", h=4, two=2)
                            rB4 = rB[:].rearrange("p (h two d) -> p h two d", h=4, two=2)
                            rA4 = rA[:].rearrange("p (h two d) -> p h two d", h=4, two=2)
                            cosb = rot[:, s * 128:s * 128 + 64].unsqueeze(1).unsqueeze(1).to_broadcast([128, 4, 2, 64])
                            sinb = rot[:, s * 128 + 64:s * 128 + 128].unsqueeze(1).to_broadcast([128, 4, 64])
                            S.op("dve", f_tt(rA4, ps4, cosb, ALU.mult), reads=[PK(b), "rot"], writes=["rA"])
                            S.op("dve", f_tt(rB4[:, :, 0, :], ps4[:, :, 1, :], sinb, ALU.mult), reads=[PK(b), "rot"], writes=["rB0"])
                            S.op("dve", f_tt(rB4[:, :, 1, :], ps4[:, :, 0, :], sinb, ALU.mult), reads=[PK(b), "rot"], writes=["rB1"])
                            yield
                            if u == 0:
                                q4 = qkr[:, s, 0:512].rearrange("p (h two d) -> p h two d", h=4, two=2)
                                S.op("pool", f_tt(q4[:, :, 0, :], rA4[:, :, 0, :], rB4[:, :, 0, :], ALU.subtract), reads=["rA", "rB0"], writes=[("qr", s)])
                                S.op("pool", f_tt(q4[:, :, 1, :], rA4[:, :, 1, :], rB4[:, :, 1, :], ALU.add), reads=["rA", "rB1"], writes=[("qr", s)])
                            else:
                                S.op("pool", f_tt(rA4[:, :, 0, :], rA4[:, :, 0, :], rB4[:, :, 0, :], ALU.subtract), reads=["rA", "rB0"], writes=["rA"])
                                S.op("pool", f_tt(rA4[:, :, 1, :], rA4[:, :, 1, :], rB4[:, :, 1, :], ALU.add), reads=["rA", "rB1"], writes=["rA"])
                                yield
                                S.op("act", f_acopy(qkr[:, s, 512:1024], rA[:]), reads=["rA"], writes=[("kr", s)])
                                decb = small[:, 0:4].unsqueeze(2).to_broadcast([128, 4, 128])
                                S.op("pool", f_tt(kdec[:, s, :].rearrange("p (h d) -> p h d", h=4),
                                                  rA[:].rearrange("p (h d) -> p h d", h=4), decb, ALU.mult),
                                     reads=["rA", "small"], writes=[("kdec", s)])
                        else:
                            vh = u - 2
                            S.op("act", f_acopy(vb[:, s, vh * 512:(vh + 1) * 512], pb[b][:]), reads=[PK(b)], writes=[("vb", s, vh)])
                        yield

            def Cthread(m, pool):
                for hp in range(2):
                    yield from Cpass(m, hp, pool)

            def Cpass(m, hp, pool):
                hTm = hT[m % 3]
                retTm = retT[m % 2]
                qT_, sT_, gs_, gm_, gr_, gn_, xh_ = qkT[0], sTm[0], gst[0], gmv[0], grs[0], gnm[0], xhat[0]
                K = lambda n: (n, 0)
                tgb, ynb, tgk, ynk = rA, rB, ["rA"], ["rB0", "rB1"]
                for s in range(NS):
                    bT = pool.get()
                    pT = pb[bT][:].bitcast(BF16)
                    for j in range(4):
                        hh = hp * 2 + (j % 2)
                        col = (0 if j < 2 else 512) + hh * 128
                        S.op("pe", f_tr(pT[:, j * 128:(j + 1) * 128], qkr[:, s, col:col + 128], identb[:]),
                             reads=[("qr", s) if j < 2 else ("kr", s), "identb"], writes=[PK(bT)], inc=(j == 3))
                    yield
                    S.op("act", f_acopy(qT_[:].rearrange("p a b -> p (a b)"), pT[:, 0:512]), reads=[PK(bT)], writes=[K("qkT")])
                    yield
                    bS = pool.get()
                    for hl in range(2):
                        S.op("pe", f_mm(pb[bS][:, hl * 128:(hl + 1) * 128], qT_[:, 2 + hl, :], qT_[:, hl, :], True, True),
                             reads=[K("qkT")], writes=[PK(bS)], inc=(hl == 1))
                    bK = pool.get()
                    for hl in range(2):
                        hh = hp * 2 + hl
                        S.op("pe", f_mm(pb[bK][:, hl * 256:(hl + 1) * 256], kdec[:, s, hh * 128:(hh + 1) * 128],
                                        vb[:, s, hh * 256:(hh + 1) * 256], True, True),
                             reads=[("kdec", s), ("vb", s, hp)], writes=[PK(bK)], inc=(hl == 1))
                    yield
                    S.op("dve", f_tt(sT_[:].rearrange("p a b -> p (a b)"), pb[bS][:, 0:256],
                                     maskT[:, hp * 256:(hp + 1) * 256], ALU.mult),
                         reads=[PK(bS), "maskT"], writes=[K("sTm")])
                    yield
                    bO = pool.get()
                    for hl in range(2):
                        hh = hp * 2 + hl
                        S.op("pe", f_mm(pb[bO][:, hl * 256:(hl + 1) * 256], sT_[:, hl, :], vb[:, s, hh * 256:(hh + 1) * 256], True, False),
                             reads=[K("sTm"), ("vb", s, hp)], writes=[PK(bO)], inc=False)
                        S.op("pe", f_mm(pb[bO][:, hl * 256:(hl + 1) * 256], qT_[:, hl, :], Sb[:, hh * 256:(hh + 1) * 256], False, True),
                             reads=[K("qkT"), ("Sb", hp)], writes=[PK(bO)], inc=(hl == 1))
                    yield
                    for hl in range(2):
                        hh = hp * 2 + hl
                        S.op("dve", f_stt(S32[:, hh * 256:(hh + 1) * 256], S32[:, hh * 256:(hh + 1) * 256], g128[hh],
                                          pb[bK][:, hl * 256:(hl + 1) * 256], ALU.mult, ALU.add),
                             reads=[PK(bK), ("S32", hh)], writes=[("S32", hh)])
                    yield
                    S.op("dve", f_copy(Sb[:, hp * 512:(hp + 1) * 512], S32[:, hp * 512:(hp + 1) * 512]),
                         reads=[("S32", hp * 2), ("S32", hp * 2 + 1)], writes=[("Sb", hp)])
                    for hl in range(2):
                        S.op("dve", lambda e, hl=hl, bO=bO: e.bn_stats(gs_[:, hl * 6:(hl + 1) * 6], pb[bO][:, hl * 256:(hl + 1) * 256]),
                             reads=[PK(bO)], writes=[K(("gst", hl))])
                        S.op("dve", lambda e, hl=hl: e.bn_aggr(gm_[:, hl, :], gs_[:, hl * 6:(hl + 1) * 6]),
                             reads=[K(("gst", hl))], writes=[K(("gmv", hl))])
                    yield
                    S.op("pool", f_tt(gr_[:], gm_[:, :, 1], small[:, 4 + hp * 2:6 + hp * 2], ALU.add),
                         reads=[K(("gmv", 0)), K(("gmv", 1)), "small"], writes=[K("grs")])
                    S.op("pool", f_tt(gr_[:], gr_[:], mhalf[:, 0:2], ALU.pow), reads=[K("grs"), "mhalf"], writes=[K("grs")])
                    S.op("pool", f_tt(gn_[:], gm_[:, :, 0], gr_[:], ALU.mult), reads=[K(("gmv", 0)), K(("gmv", 1)), K("grs")], writes=[K("gnm")])
                    S.op("pool", f_ts(gn_[:], gn_[:], -1.0, None, ALU.mult), reads=[K("gnm")], writes=[K("gnm")])
                    yield
                    for hl in range(2):
                        S.op("act", f_act(xh_[:, s, hl * 256:(hl + 1) * 256], pb[bO][:, hl * 256:(hl + 1) * 256], AF.Identity,
                                          bias=gn_[:, hl:hl + 1], scale=gr_[:, hl:hl + 1]),
                             reads=[PK(bO), K("grs"), K("gnm")], writes=[("xhat", 0, s, hl)])
                    yield
                for cl in range(4):
                    c = hp * 4 + cl
                    slot = use(m, 4 + hp)
                    rk = ("ring", slot)
                    bG = pool.get()
                    for kc in range(8):
                        S.op("pe", f_mm(pb[bG][:], slab(slot, cl, kc), hTm[:, kc, :], kc == 0, kc == 7),
                             reads=hT_all(m, kc) + [rk], writes=[PK(bG)], inc=(kc == 7))
                    if cl == 3:
                        done(m, 4 + hp)
                    bX = pool.get()
                    for s in range(NS):
                        S.op("pe", f_tr(pb[bX][:, s * 128:(s + 1) * 128], xh_[:, s, cl * 128:(cl + 1) * 128], identf[:]),
                             reads=[("xhat", 0, s, cl // 2), "identf"], writes=[PK(bX)], inc=(s == NS - 1))
                    yield
                    S.op("act", f_act(tgb[:], pb[bG][:], AF.Tanh, scale=0.5), reads=[PK(bG)], writes=tgk)
                    S.op("act", f_act(ynb[:], pb[bX][:], AF.Identity, bias=dq[:, Q_HGNB, c:c + 1], scale=dq[:, Q_HGNG, c:c + 1]),
                         reads=[PK(bX), "dq6", "dq7"], writes=ynk)
                    yield
                    S.op("dve", f_stt(tgb[:], tgb[:], 1.0, pb[bG][:], ALU.add, ALU.mult), reads=tgk + [PK(bG)], writes=tgk)
                    yield
                    S.op("pool", f_tt(retTm[:, c, :], ynb[:], tgb[:], ALU.mult), reads=ynk + tgk, writes=[("retT", m % 2, c)])
                    yield

            def Dthread(m, pool):
                hTm = hT[m % 3]
                ucount = {}

                def front(c):
                    u = 6 + c // 2
                    slot = use(m, u)
                    rk = ("ring", slot)
                    sl = (c % 2) * 2
                    di = c % 2
                    bXL = pool.get()
                    for kc in range(8):
                        S.op("pe", f_mm(pb[bXL][:], slab(slot, sl, kc), hTm[:, kc, :], kc == 0, kc == 7),
                             reads=hT_all(m, kc) + [rk], writes=[PK(bXL)], inc=(kc == 7))
                    xlk, xck, xcbk = ("xl", di), ("xc", di), ("xcb", di)
                    S.op("pool", f_copy(xl[di][:, 0:3], hist[:, c, :]), reads=[("hist", c)], writes=[xlk])
                    yield
                    S.op("act", f_acopy(xl[di][:, 3:515], pb[bXL][:]), reads=[PK(bXL)], writes=[xlk])
                    S.op("act", f_act(xc[di][:], pb[bXL][:], AF.Identity, bias=vecs[:, V_CB, c:c + 1], scale=vecs[:, V_CW3, c:c + 1]),
                         reads=[PK(bXL), "vecs"], writes=[xck])
                    bGL = pool.get()
                    for kc in range(8):
                        S.op("pe", f_mm(pb[bGL][:], slab(slot, sl + 1, kc), hTm[:, kc, :], kc == 0, kc == 7),
                             reads=hT_all(m, kc) + [rk], writes=[PK(bGL)], inc=(kc == 7))
                    ucount[u] = ucount.get(u, 0) + 1
                    if ucount[u] == 2:
                        done(m, u)
                    yield
                    S.op("pool", f_copy(hist[:, c, :], xl[di][:, 512:515]), reads=[xlk], writes=[("hist", c)])
                    for t, vi in ((2, V_CW2), (1, V_CW1), (0, V_CW0)):
                        S.op("dve", f_stt(xc[di][:], xl[di][:, t:t + 512], vecs[:, vi, c:c + 1], xc[di][:], ALU.mult, ALU.add),
                             reads=[xlk, xck, "vecs"], writes=[xck])
                    S.op("act", f_act(tgl[di][:], pb[bGL][:], AF.Tanh, scale=0.5), reads=[PK(bGL)], writes=[("tgl", di)])
                    yield
                    S.op("dve", f_copy(xcb[di][:], xc[di][:]), reads=[xck], writes=[xcbk])
                    S.op("dve", f_stt(tgl[di][:], tgl[di][:], 1.0, pb[bGL][:], ALU.add, ALU.mult), reads=[("tgl", di), PK(bGL)], writes=[("tgl", di)])
                    yield
                    bA = pool.get()
                    S.op("pe", f_mm(pb[bA][:], rgw[:, 0, c, :], xcb[di][:], True, True), reads=["rgw", xcbk], writes=[PK(bA)])
                    bI = pool.get()
                    S.op("pe", f_mm(pb[bI][:], rgw[:, 1, c, :], xcb[di][:], True, True), reads=["rgw", xcbk], writes=[PK(bI)])
                    S.op("act", f_act(ta[di][:], pb[bA][:], AF.Tanh, bias=dq[:, Q_HBA, c:c + 1], scale=0.5), reads=[PK(bA), "dq0"], writes=[("ta", di)])
                    S.op("act", f_act(ti[di][:], pb[bI][:], AF.Tanh, bias=dq[:, Q_HBX, c:c + 1], scale=0.5), reads=[PK(bI), "dq1"], writes=[("ti", di)])
                    yield
                    S.op("act", f_act(av[di][:], ta[di][:], AF.Exp, bias=dq[:, Q_NHSP, c:c + 1], scale=dq[:, Q_NHSP, c:c + 1]),
                         reads=[("ta", di), "dq4"], writes=[("av", di)])
                    S.op("act", f_act(ta[di][:], ta[di][:], AF.Tanh, bias=dq[:, Q_QSP, c:c + 1], scale=dq[:, Q_QSP, c:c + 1]),
                         reads=[("ta", di), "dq5"], writes=[("ta", di)])
                    S.op("dve", f_stt(ti[di][:], ti[di][:], 1.0, xc[di][:], ALU.add, ALU.mult), reads=[("ti", di), xck], writes=[("ti", di)])
                    yield

                def back(c):
                    di = c % 2
                    S.op("act", f_act(ta[di][:], ta[di][:], AF.Sqrt, scale=1.0 / 16.0), reads=[("ta", di)], writes=[("ta", di)])
                    yield
                    S.op("dve", f_stt(ta[di][:], av[di][:], 1.0, ta[di][:], ALU.add, ALU.mult), reads=[("av", di), ("ta", di)], writes=[("ta", di)])
                    yield
                    S.op("dve", f_tt(ta[di][:], ta[di][:], ti[di][:], ALU.mult), reads=[("ta", di), ("ti", di)], writes=[("ta", di)])
                    yield
                    S.op("dve", lambda e, c=c, di=di: e.tensor_tensor_scan(ti[di][:], av[di][:], ta[di][:], carry[:, c:c + 1], ALU.mult, ALU.add),
                         reads=[("av", di), ("ta", di), ("carry", c)], writes=[("ti", di)])
                    yield
                    S.op("pool", f_copy(carry[:, c:c + 1], ti[di][:, 511:512]), reads=[("ti", di)], writes=[("carry", c)])
                    S.op("pool", f_tt(lruT[m % 2][:, c, :], ti[di][:], tgl[di][:], ALU.mult), reads=[("ti", di), ("tgl", di)], writes=[("lruT", m % 2, c)])
                    yield

                for c0 in range(0, 8, 2):
                    yield from lockstep(front(c0), front(c0 + 1))
                    yield from lockstep(back(c0), back(c0 + 1))

            def Ethread(m, pool):
                hTm = hT[m % 3]
                for c in range(8):
                    u = 10 + c
                    slot = use(m, u)
                    rk = ("ring", slot)
                    ei = 0
                    bMR = pool.get()
                    for kc in range(8):
                        S.op("pe", f_mm(pb[bMR][:], slab(slot, 0, kc), hTm[:, kc, :], kc == 0, kc == 7),
                             reads=hT_all(m, kc) + [rk], writes=[PK(bMR)], inc=(kc == 7))
                    yield
                    bML = pool.get()
                    for kc in range(8):
                        S.op("pe", f_mm(pb[bML][:], slab(slot, 1, kc), hTm[:, kc, :], kc == 0, kc == 7),
                             reads=hT_all(m, kc) + [rk], writes=[PK(bML)], inc=(kc == 7))
                    S.op("act", f_act(tr_[ei][:], pb[bMR][:], AF.Tanh, bias=dq[:, Q_HBM0, c:c + 1], scale=0.5), reads=[PK(bMR), "dq2"], writes=[("tr", ei)])
                    yield
                    bPR = pool.get()
                    for kc in range(8):
                        S.op("pe", f_mm(pb[bPR][:], slab(slot, 2, kc), retT[m % 2][:, kc, :], kc == 0, kc == 7),
                             reads=[("retT", m % 2, kc), rk], writes=[PK(bPR)], inc=(kc == 7))
                    S.op("act", f_act(tl_[ei][:], pb[bML][:], AF.Tanh, bias=dq[:, Q_HBM1, c:c + 1], scale=0.5), reads=[PK(bML), "dq3"], writes=[("tl", ei)])
                    yield
                    bPL = pool.get()
                    for kc in range(8):
                        S.op("pe", f_mm(pb[bPL][:], slab(slot, 3, kc), lruT[m % 2][:, kc, :], kc == 0, kc == 7),
                             reads=[("lruT", m % 2, kc), rk], writes=[PK(bPL)], inc=(kc == 7))
                    done(m, u)
                    S.op("dve", f_stt(tr_[ei][:], tr_[ei][:], 1.0, pb[bPR][:], ALU.add, ALU.mult), reads=[("tr", ei), PK(bPR)], writes=[("tr", ei)])
                    yield
                    S.op("dve", f_stt(tl_[ei][:], tl_[ei][:], 1.0, pb[bPL][:], ALU.add, ALU.mult), reads=[("tl", ei), PK(bPL)], writes=[("tl", ei)])
                    yield
                    S.op("pool", f_tt(mrg[:, c, :], tr_[ei][:], tl_[ei][:], ALU.add), reads=[("tr", ei), ("tl", ei)], writes=[("mrg", c)])
                    yield

            def Fthread(m, pool):
                load_h(m, 0)
                for s in range(NS):
                    ob = obuf[s % 2]
                    ok = ("obuf", s % 2)
                    hl = hland[0]
                    hlk = ("hA", 0)
                    bY = [pool.get(), pool.get()]
                    for hf in range(2):
                        slot = use(m, 18 + hf)
                        for kc in range(8):
                            S.op("pe", f_mm(pb[bY[hf]][:], mrg[:, kc, s * 128:(s + 1) * 128], tmw(slot, kc), kc == 0, kc == 7),
                                 reads=[("mrg", kc), ("ring", slot)], writes=[PK(bY[hf])], inc=(kc == 7))
                    if s == NS - 1:
                        done(m, 18)
                        done(m, 19)
                    yield
                    for hf in range(2):
                        S.op("dve", f_stt(ob[:, hf * 512:(hf + 1) * 512], hl[:, hf * 512:(hf + 1) * 512], 2.0 * ALPHA,
                                          pb[bY[hf]][:], ALU.mult, ALU.add),
                             reads=[hlk, PK(bY[hf])], writes=[ok])
                    if s + 1 < NS:
                        load_h(m, s + 1)
                    S.op("dve", lambda e, ob=ob: e.bn_stats(lst2[:, 0:6], ob[:, 0:512]), reads=[ok], writes=["lst20"])
                    S.op("dve", lambda e, ob=ob: e.bn_stats(lst2[:, 6:12], ob[:, 512:1024]), reads=[ok], writes=["lst21"])
                    S.op("dve", lambda e: e.bn_aggr(lmv2[:], lst2[:]), reads=["lst20", "lst21"], writes=["lmv2"])
                    yield
                    S.op("pool", f_ts(lrs2[:], lmv2[:, 1:2], 4.0 * LN_EPS, None, ALU.add), reads=["lmv2"], writes=["lrs2"])
                    S.op("pool", f_tt(lrs2[:], lrs2[:], mhalf[:, 0:1], ALU.pow), reads=["lrs2", "mhalf"], writes=["lrs2"])
                    yield
                    S.op("dve", f_stt(ob[:], ob[:], lmv2[:, 0:1], lnog[:], ALU.subtract, ALU.mult), reads=[ok, "lmv2", "lnog"], writes=[ok])
                    S.op("dve", f_stt(ob[:], ob[:], lrs2[:], lnob[:], ALU.mult, ALU.add), reads=[ok, "lrs2", "lnob"], writes=[ok])
                    t0 = m * TM + s * 128
                    S.dma("pool", "os%d" % (s % 2), out_d[t0:t0 + 128, :], ob[:], reads=[ok])
                    yield

            def chain(*gs):
                for g in gs:
                    yield from g

            run_gen(Athread(0, BankPool(S, [6, 7])))
            run_gen(Bthread(0, BankPool(S, [4, 5, 6, 7])))
            th = [(Cthread(0, BankPool(S, [0, 1])), 2), (Dthread(0, BankPool(S, [2, 3, 4])), 1)]
            if n_macro > 1:
                th.append((Athread(1, BankPool(S, [5])), 1))
            interleave(th)
            for m in range(n_macro):
                th = []
                if m + 1 < n_macro:
                    th.append((Bthread(m + 1, BankPool(S, [4, 5, 6, 7])), 2))
                if m >= 1:
                    th.append((Fthread(m - 1, BankPool(S, [0, 1, 2, 3])), 1))
                interleave(th)
                pEA = BankPool(S, [5, 6, 7])
                th = [(Ethread(m, pEA), 1)]
                if m + 1 < n_macro:
                    th.append((Cthread(m + 1, BankPool(S, [0, 1])), 2))
                    th.append((Dthread(m + 1, BankPool(S, [2, 3, 4])), 1))
                if m + 2 < n_macro:
                    th.append((Athread(m + 2, pEA), 1))
                interleave(th)
            run_gen(Fthread(n_macro - 1, BankPool(S, [0, 1, 2, 3])))
            S.wait_all("pool", [("obuf", 0), ("obuf", 1)] + [("hscr", n_macro - 1, s) for s in range(NS)])
            S.wait_all("sp", [("ring", i) for i in range(RING)] + [("hA", 0), "rot"])
            S.wait_all("act", [("ring", i) for i in range(RING)] + [("wscr", u) for u in range(NU)])

        record = []
        emit_all(Sched(nc, None, dry=True), None, record)
        S = Sched(nc, st)
        emit_all(S, record, None)
        with nc.Block() as block:
            S.emit(block)
    return nc


def _tm_unit(W, col0):
    blk = W[:, col0:col0 + 512].reshape(8, 128, 512).transpose(1, 0, 2)
    return np.ascontiguousarray(blk).reshape(128, 4096)


def _slab_unit(slabs):
    out = np.empty((128, 4, 8, 128), np.float32)
    for i, (W, col0) in enumerate(slabs):
        out[:, i] = W[:, col0:col0 + 128].reshape(8, 128, 128).transpose(1, 0, 2)
    return out.reshape(128, 4096)


def _fm(v):
    return np.ascontiguousarray(np.asarray(v, np.float32).reshape(8, 128).T)


def _const_tables():
    half = 64
    inv = (np.float32(10000.0) ** (np.float32(-2.0) * np.arange(half, dtype=np.float32) / np.float32(128))).astype(np.float32)
    ang = (np.arange(SEQ, dtype=np.float32)[:, None] * inv[None, :]).astype(np.float32)
    cos = np.cos(ang.astype(np.float64)).astype(np.float32)
    sin = np.sin(ang.astype(np.float64)).astype(np.float32)
    tab = np.concatenate([cos, sin], axis=1)
    rot = tab.reshape(NM, NS, 128, 128).transpose(0, 2, 1, 3).reshape(NM, 128, NS * 128)
    rot = np.ascontiguousarray(rot)
    lg = np.log1p(-np.exp2(-5.0 - np.arange(4, dtype=np.float64)))
    i = np.arange(128, dtype=np.float64)[:, None]
    j = np.arange(128, dtype=np.float64)[None, :]
    ci = (i // 64)
    cj = (j // 64)
    maskT = np.zeros((128, 4, 128), np.float64)
    for h in range(4):
        e = np.where(ci == cj, np.abs(i - j), i - j) - (i + 1.0)
        mk = np.where(cj > ci, 0.0, np.exp(lg[h] * e)) * (128.0 ** -0.5)
        maskT[:, h, :] = mk.T
    small = np.zeros((128, 8), np.float64)
    jj = np.arange(128, dtype=np.float64)
    for h in range(4):
        small[:, h] = (128.0 ** -0.5) * np.exp(lg[h] * (127.0 - jj))
        small[:, 4 + h] = LN_EPS / np.exp(2.0 * lg[h] * (jj + 1.0))
    return rot, maskT.reshape(128, 512).astype(np.float32), small.astype(np.float32)


def _prep_shared(inp):
    W = np.asarray(inp["w_in"], np.float32)[0]
    Pr = np.asarray(inp["w_ret_proj"], np.float32)[0]
    Pl = np.asarray(inp["w_lru_proj"], np.float32)[0]
    Wo = np.asarray(inp["w_out"], np.float32)[0]
    units = []
    units.append(_tm_unit(W, 0))
    units.append(_tm_unit(W, 512))
    units.append(_tm_unit(W, 1024))
    units.append(_tm_unit(W, 1536))
    for hp in range(2):
        units.append(_slab_unit([(W, 2048 + (hp * 4 + cl) * 128) for cl in range(4)]))
    for jd in range(4):
        c0, c1 = 2 * jd, 2 * jd + 1
        units.append(_slab_unit([(W, 3072 + c0 * 128), (W, 4096 + c0 * 128), (W, 3072 + c1 * 128), (W, 4096 + c1 * 128)]))
    for c in range(8):
        units.append(_slab_unit([(W, 5120 + c * 128), (W, 6144 + c * 128), (Pr, c * 128), (Pl, c * 128)]))
    units.append(_tm_unit(Wo, 0))
    units.append(_tm_unit(Wo, 512))
    wpack = np.stack(units, 0)
    wa = np.asarray(inp["w_rg_a"], np.float32)[0]
    wx = np.asarray(inp["w_rg_x"], np.float32)[0]
    rgw = np.stack([wa.transpose(1, 0, 2), wx.transpose(1, 0, 2)], axis=1)
    rgw = np.ascontiguousarray(rgw).reshape(128, 2048)
    cw = np.asarray(inp["conv_w"], np.float32)[0]
    bm = np.asarray(inp["b_merge"], np.float32)[0]
    vl = [cw[0], cw[1], cw[2], cw[3], inp["conv_b"][0], inp["b_rg_a"][0], inp["b_rg_x"][0], inp["lru_lambda"][0],
          bm[0], bm[1], inp["ret_gn_g"][0], inp["ret_gn_b"][0]]
    vecs = np.stack([_fm(v) for v in vl], axis=1).reshape(128, NV * 8)
    lnv = np.stack([np.asarray(inp["ln_in_g"], np.float32), np.asarray(inp["ln_in_b"], np.float32),
                    np.asarray(inp["ln_out_g"], np.float32)[0], np.asarray(inp["ln_out_b"], np.float32)[0]], 0)
    rot, maskT, small = _const_tables()
    return {"wpack": wpack, "rgw": rgw, "vecs": np.ascontiguousarray(vecs), "lnv": np.ascontiguousarray(lnv),
            "rot": rot, "maskT": maskT, "small": small, "ident": np.eye(128, dtype=np.float32)}


def kernel(**inputs):
    x = np.asarray(inputs["x"], np.float32)
    shared = _prep_shared(inputs)
    nc = build_program()
    in_maps = []
    for b in range(NCORES):
        d = dict(shared)
        d["x"] = np.ascontiguousarray(x[b])
        in_maps.append(d)
    res = run_bass_kernel_spmd(nc, in_maps, core_ids=list(range(NCORES)))
    out = np.stack([np.asarray(res.results[b]["out"], np.float32) for b in range(NCORES)], 0)
    return out
```

```python
import contextlib
import numpy as np
import concourse.bass as bass
import concourse.mybir as mybir
from concourse.bass_utils import run_bass_kernel_spmd

F32 = mybir.dt.float32
BF16 = mybir.dt.bfloat16
AF = mybir.ActivationFunctionType
ALU = mybir.AluOpType

D = 1024
SEQ = 8192
NCORES = 8
TM = 512
NM = SEQ // TM
NS = TM // 128
NU = 20
RING = 4
LN_EPS = 1e-5
ALPHA = 2.0 ** 0.25
BETA = 8.0 ** -0.25
ENGS = ("pe", "act", "dve", "pool", "sp")

V_CW0, V_CW1, V_CW2, V_CW3, V_CB, V_BA, V_BX, V_LAM, V_BM0, V_BM1, V_GNG, V_GNB = range(12)
NV = 12
Q_HBA, Q_HBX, Q_HBM0, Q_HBM1, Q_NHSP, Q_QSP, Q_HGNG, Q_HGNB, Q_T0, Q_T1 = range(10)
NQ = 10


class Sched:
    def __init__(self, nc, stack, dry=False):
        self.nc = nc
        self.stack = stack
        self.dry = dry
        self.sem = {}
        if not dry:
            for e in ENGS:
                self.sem["E_" + e] = stack.enter_context(nc.semaphore("E_" + e))
        self.cnt = {e: 0 for e in ENGS}
        self.dcnt = {}
        self.waited = {e: {} for e in ENGS}
        self.lastw = {}
        self.readers = {}
        self.ops = {e: [] for e in ENGS}
        self.eng_free = {e: 0.0 for e in ENGS}
        self.fin = {}
        self.step_fin = 0.0
        self.dep_fin = 0.0

    def dsem(self, name):
        if name not in self.dcnt:
            if not self.dry:
                self.sem[name] = self.stack.enter_context(self.nc.semaphore(name))
            self.dcnt[name] = 0
        return name

    def _deps(self, eng, reads, writes):
        deps = {}
        self.dep_fin = 0.0

        def add(rec):
            s, v, src = rec
            f = self.fin.get((s, v), 0.0)
            if f > self.dep_fin:
                self.dep_fin = f
            if src == eng and eng == "pe":
                return
            if deps.get(s, 0) < v:
                deps[s] = v
        for k in reads:
            if k in self.lastw:
                add(self.lastw[k])
        for k in writes:
            if k in self.lastw:
                add(self.lastw[k])
            for s, (v, src) in self.readers.get(k, {}).items():
                add((s, v, src))
        waits = []
        for s, v in deps.items():
            if self.waited[eng].get(s, 0) >= v:
                continue
            self.waited[eng][s] = v
            waits.append((s, v))
        return waits

    def _record(self, rec, reads, writes):
        s, v, src = rec
        for k in reads:
            d = self.readers.setdefault(k, {})
            if d.get(s, (0, None))[0] < v:
                d[s] = (v, src)
        for k in writes:
            prev = self.lastw.get(k)
            assert not (prev is not None and prev[2].startswith("dma:") and not self.readers.get(k)), \
                ("DMA-loaded buffer overwritten before any consumer was emitted", k)
            self.lastw[k] = rec
            self.readers[k] = {}

    COST = {"pe": 0.25, "act": 0.65, "dve": 0.6, "pool": 0.9, "sp": 0.1}

    def op(self, eng, fn, reads=(), writes=(), inc=True, cost=None):
        waits = self._deps(eng, reads, writes)
        if inc:
            self.cnt[eng] += 1
            v = self.cnt[eng]
        else:
            v = self.cnt[eng] + 1
        c = self.COST[eng] if cost is None else cost
        start = max(self.eng_free[eng], self.dep_fin + (0.0 if eng == "pe" else 0.2))
        fin = start + c
        self.eng_free[eng] = fin
        self.fin[("E_" + eng, v)] = max(fin, self.fin.get(("E_" + eng, v), 0.0))
        if fin > self.step_fin:
            self.step_fin = fin
        self._record(("E_" + eng, v, eng), reads, writes)
        self.ops[eng].append((waits, fn, ("E_" + eng, 1) if inc else None))

    def dma(self, q, dsem, out, in_, reads=(), writes=(), **kw):
        self.dsem(dsem)
        waits = self._deps(q, reads, writes)
        self.dcnt[dsem] += 16
        v = self.dcnt[dsem]
        lat = kw.pop("lat", 4.0)
        start = max(self.eng_free[q], self.dep_fin + 0.2)
        self.eng_free[q] = start + 0.1
        self.fin[(dsem, v)] = start + lat
        self._record((dsem, v, "dma:" + dsem), reads, writes)
        self.ops[q].append((waits, lambda e, o=out, i=in_: e.dma_start(out=o, in_=i, **kw), (dsem, 16)))

    def wait_all(self, eng, keys):
        waits = self._deps(eng, (), keys)
        self.ops[eng].append((waits, None, None))

    def emit(self, block):
        me = self

        def run(eng_name, e):
            for waits, fn, inc in me.ops[eng_name]:
                for s, v in waits:
                    e.wait_ge(me.sem[s], v)
                if fn is None:
                    continue
                inst = fn(e)
                if inc is not None:
                    inst.then_inc(me.sem[inc[0]], inc[1])

        @block.tensor
        def _(e):
            run("pe", e)

        @block.scalar
        def _(e):
            run("act", e)

        @block.vector
        def _(e):
            run("dve", e)

        @block.gpsimd
        def _(e):
            run("pool", e)

        @block.sync
        def _(e):
            run("sp", e)


def f_act(out, in_, func, bias=None, scale=None):
    kw = {}
    if bias is not None:
        kw["bias"] = bias
    if scale is not None:
        kw["scale"] = scale
    return lambda e: e.activation(out, in_, func, **kw)


def f_tt(out, a, b, op):
    return lambda e: e.tensor_tensor(out, a, b, op)


def f_stt(out, a, sc, b, op0, op1):
    return lambda e: e.scalar_tensor_tensor(out, a, sc, b, op0, op1)


def f_ts(out, a, s1, s2, op0, op1=None):
    if op1 is None:
        return lambda e: e.tensor_scalar(out, a, s1, None, op0)
    return lambda e: e.tensor_scalar(out, a, s1, s2, op0, op1)


def f_copy(out, in_):
    return lambda e: e.tensor_copy(out, in_)


def f_acopy(out, in_):
    return lambda e: e.copy(out, in_)


def f_mm(out, lhsT, rhs, start, stop):
    return lambda e: e.matmul(out, lhsT, rhs, start=start, stop=stop)


def f_tr(out, in_, ident):
    return lambda e: e.transpose(out, in_, ident)


def f_memset(out, val):
    return lambda e: e.memset(out, val)


class BankPool:
    def __init__(self, S, banks):
        self.S = S
        self.banks = list(banks)
        self.i = 0

    def get(self):
        for _ in range(len(self.banks)):
            b = self.banks[self.i % len(self.banks)]
            self.i += 1
            key = "pb%d" % b
            if key not in self.S.lastw or self.S.readers.get(key):
                return b
        raise AssertionError(("no PSUM bank with emitted consumers", self.banks))


def interleave(gens_w):
    gens = [[g, w, True] for g, w in gens_w]
    while any(a for _, _, a in gens):
        for ent in gens:
            if not ent[2]:
                continue
            for _ in range(ent[1]):
                try:
                    next(ent[0])
                except StopIteration:
                    ent[2] = False
                    break


def schedule(S, gens):
    ready = [0.0 for _ in gens]
    alive = [True for _ in gens]
    base = max(S.eng_free.values()) if False else 0.0
    while any(alive):
        i = min((k for k in range(len(gens)) if alive[k]), key=lambda k: ready[k])
        S.step_fin = ready[i]
        try:
            next(gens[i])
            ready[i] = S.step_fin
        except StopIteration:
            alive[i] = False


def interleave_bg(gens_w, bg_w):
    gens = [[g, w, True] for g, w in gens_w]
    bgs = [[g, w, True] for g, w in bg_w]
    while any(a for _, _, a in gens):
        for ent in gens + bgs:
            if not ent[2]:
                continue
            for _ in range(ent[1]):
                try:
                    next(ent[0])
                except StopIteration:
                    ent[2] = False
                    break


def run_gen(g):
    for _ in g:
        pass


def lockstep(*gs):
    alive = list(gs)
    while alive:
        nxt = []
        for g in alive:
            try:
                next(g)
                nxt.append(g)
            except StopIteration:
                pass
        alive = nxt
        if alive:
            yield


def build_program(n_macro=NM):
    nc = bass.Bass("TRN2", target_bir_lowering=False)
    x_d = nc.dram_tensor("x", [SEQ, D], F32, kind="ExternalInput").ap()
    wpack_d = nc.dram_tensor("wpack", [NU, 128, 4096], F32, kind="ExternalInput").ap()
    rgw_d = nc.dram_tensor("rgw", [128, 2048], F32, kind="ExternalInput").ap()
    vecs_d = nc.dram_tensor("vecs", [128, NV * 8], F32, kind="ExternalInput").ap()
    lnv_d = nc.dram_tensor("lnv", [4, D], F32, kind="ExternalInput").ap()
    rot_d = nc.dram_tensor("rot", [NM, 128, NS * 128], F32, kind="ExternalInput").ap()
    mask_d = nc.dram_tensor("maskT", [128, 512], F32, kind="ExternalInput").ap()
    small_d = nc.dram_tensor("small", [128, 8], F32, kind="ExternalInput").ap()
    ident_d = nc.dram_tensor("ident", [128, 128], F32, kind="ExternalInput").ap()
    out_d = nc.dram_tensor("out", [SEQ, D], F32, kind="ExternalOutput").ap()
    wscr_d = nc.dram_tensor("wscr", [NU, 128, 4096], BF16, kind="Internal").ap()
    hscr_d = nc.dram_tensor("hscr", [SEQ, D], F32, kind="Internal").ap()

    lg = [float(np.log1p(-2.0 ** (-5.0 - h))) for h in range(4)]
    g128 = [float(np.exp(128.0 * lg[h])) for h in range(4)]

    with contextlib.ExitStack() as st:
        def T(name, shape, dt):
            return st.enter_context(nc.sbuf_tensor(name, shape, dt))

        identf = T("identf", [128, 128], F32)
        identb = T("identb", [128, 128], BF16)
        lnig = T("lnig", [128, D], F32)
        lnib = T("lnib", [128, D], F32)
        lnog = T("lnog", [128, D], F32)
        lnob = T("lnob", [128, D], F32)
        maskT = T("maskT_sb", [128, 512], F32)
        small = T("small_sb", [128, 8], F32)
        vecs = T("vecs_sb", [128, NV, 8], F32)
        dq = T("dq", [128, NQ, 8], F32)
        mhalf = T("mhalf", [128, 8], F32)
        rgw = T("rgw_sb", [128, 2, 8, 128], BF16)
        ring = [T("ring%d" % i, [128, 4096], BF16) for i in range(RING)]
        obuf = [T("obuf%d" % i, [128, D], F32) for i in range(2)]
        xs = obuf
        hA = [T("hA%d" % i, [128, D], F32) for i in range(1)]
        hland = hA
        hT = [T("hT%d" % i, [128, 8, TM], BF16) for i in range(3)]
        rot = T("rot_sb", [128, NS * 128], F32)
        qkr = T("qkr", [128, NS, 1024], BF16)
        kdec = T("kdec", [128, NS, 512], BF16)
        vb = T("vb", [128, NS, 1024], BF16)
        rA = T("rA", [128, 512], F32)
        rB = T("rB", [128, 512], F32)
        qkT = [T("qkT%d" % i, [128, 4, 128], BF16) for i in range(1)]
        sTm = [T("sTm%d" % i, [128, 2, 128], BF16) for i in range(1)]
        S32 = T("S32", [128, 1024], F32)
        Sb = T("Sb", [128, 1024], BF16)
        xhat = [T("xhat%d" % i, [128, NS, 512], F32) for i in range(1)]
        gst = [T("gst%d" % i, [128, 12], F32) for i in range(1)]
        gmv = [T("gmv%d" % i, [128, 2, 2], F32) for i in range(1)]
        grs = [T("grs%d" % i, [128, 2], F32) for i in range(1)]
        gnm = [T("gnm%d" % i, [128, 2], F32) for i in range(1)]
        lst = T("lst", [128, 12], F32)
        lmv = T("lmv", [128, 2], F32)
        lrs = T("lrs", [128, 1], F32)
        lst2 = T("lst2", [128, 12], F32)
        lmv2 = T("lmv2", [128, 2], F32)
        lrs2 = T("lrs2", [128, 1], F32)
        retT = [T("retT%d" % i, [128, 8, TM], BF16) for i in range(2)]
        lruT = [T("lruT%d" % i, [128, 8, TM], BF16) for i in range(2)]
        mrg = T("mrg", [128, 8, TM], BF16)
        tg = rA
        yn = rB
        xl = [T("xl%d" % i, [128, 515], F32) for i in range(2)]
        xc = [T("xc%d" % i, [128, 512], F32) for i in range(2)]
        xcb = [T("xcb%d" % i, [128, 512], BF16) for i in range(2)]
        ta = [T("ta%d" % i, [128, 512], F32) for i in range(2)]
        ti = [T("ti%d" % i, [128, 512], F32) for i in range(2)]
        av = [T("av%d" % i, [128, 512], F32) for i in range(2)]
        tgl = [T("tgl%d" % i, [128, 512], F32) for i in range(2)]
        hist = T("hist", [128, 8, 3], F32)
        carry = T("carry", [128, 8], F32)
        tr_ = [T("tr_%d" % i, [128, 512], F32) for i in range(1)]
        tl_ = [T("tl_%d" % i, [128, 512], F32) for i in range(1)]
        pb = [st.enter_context(nc.psum_tensor("pb%d" % i, [128, 512], F32)) for i in range(8)]

        def PK(i):
            return "pb%d" % i

        def emit_all(S, seq, record):
            um = {"resident": {}, "nissued": 0, "free": list(range(RING))}

            def issue_unit(mu):
                m, u = mu
                slot = um["free"].pop(0)
                um["nissued"] += 1
                um["resident"][mu] = slot
                rk = ("ring", slot)
                if m == 0:
                    S.dma("pool", "wc%d" % slot, ring[slot][:], wpack_d[u], writes=[rk], max_dma_last_dim=4096, lat=12.0)
                if m == 0:
                    S.dma("act", "ws%d" % slot, wscr_d[u], ring[slot][:], reads=[rk], writes=[("wscr", u)])
                else:
                    S.dma("sp", "wr%d" % slot, ring[slot][:], wscr_d[u], reads=[("wscr", u)], writes=[rk], lat=7.0)

            def prefetch():
                if seq is None:
                    return
                while um["nissued"] < len(seq) and um["free"]:
                    issue_unit(seq[um["nissued"]])

            def use(m, u):
                if seq is None:
                    if (m, u) not in um["resident"]:
                        um["resident"][(m, u)] = 0
                        record.append((m, u))
                    return 0
                assert (m, u) in um["resident"], ("unit not resident", m, u)
                return um["resident"][(m, u)]

            def done(m, u):
                if seq is None:
                    return
                um["free"].append(um["resident"].pop((m, u)))
                prefetch()

            def tmw(slot, kc):
                return ring[slot][:, kc * 512:(kc + 1) * 512]

            def slab(slot, sl, kc):
                o = (sl * 8 + kc) * 128
                return ring[slot][:, o:o + 128]

            S.dma("sp", "c_id", identf[:], ident_d, writes=["identf"])
            S.dma("sp", "c_lnig", lnig[:], lnv_d[0:1, :].partition_broadcast(128), writes=["lnig"])
            S.dma("sp", "c_lnib", lnib[:], lnv_d[1:2, :].partition_broadcast(128), writes=["lnib"])
            S.dma("sp", "c_lnog", lnog[:], lnv_d[2:3, :].partition_broadcast(128), writes=["lnog"])
            S.dma("sp", "c_lnob", lnob[:], lnv_d[3:4, :].partition_broadcast(128), writes=["lnob"])
            S.dma("sp", "c_mask", maskT[:], mask_d, writes=["maskT"])
            S.dma("sp", "c_small", small[:], small_d, writes=["small"])
            S.dma("sp", "c_vecs", vecs[:].rearrange("p a b -> p (a b)"), vecs_d, writes=["vecs"])
            rgflat = rgw[:].rearrange("p a b c -> p (a b c)")
            for hf in range(2):
                S.dma("sp", "c_rgw%d" % hf, obuf[hf][:], rgw_d[:, hf * 1024:(hf + 1) * 1024], writes=[("obuf", hf)])
                S.op("dve", f_copy(rgflat[:, hf * 1024:(hf + 1) * 1024], obuf[hf][:]), reads=[("obuf", hf)], writes=["rgw"])
            S.op("dve", f_copy(identb[:], identf[:]), reads=["identf"], writes=["identb"])
            S.op("pool", f_memset(mhalf[:], -0.5), writes=["mhalf"])
            S.op("pool", f_memset(S32[:], 0.0), writes=[("S32", h) for h in range(4)])
            S.op("pool", f_memset(Sb[:], 0.0), writes=[("Sb", 0), ("Sb", 1)])
            S.op("pool", f_memset(hist[:].rearrange("p a b -> p (a b)"), 0.0), writes=[("hist", c) for c in range(8)])
            S.op("pool", f_memset(carry[:], 0.0), writes=[("carry", c) for c in range(8)])
            V = lambda i: vecs[:, i, :]
            Q = lambda i: dq[:, i, :]
            S.op("dve", f_ts(Q(Q_HBA), V(V_BA), 0.5, None, ALU.mult), reads=["vecs"], writes=["dq0"])
            S.op("dve", f_ts(Q(Q_HBX), V(V_BX), 0.5, None, ALU.mult), reads=["vecs"], writes=["dq1"])
            S.op("dve", f_ts(Q(Q_HBM0), V(V_BM0), 0.5, None, ALU.mult), reads=["vecs"], writes=["dq2"])
            S.op("dve", f_ts(Q(Q_HBM1), V(V_BM1), 0.5, None, ALU.mult), reads=["vecs"], writes=["dq3"])
            S.op("dve", f_ts(Q(Q_HGNG), V(V_GNG), 0.5, None, ALU.mult), reads=["vecs"], writes=["dq6"])
            S.op("dve", f_ts(Q(Q_HGNB), V(V_GNB), 0.5, None, ALU.mult), reads=["vecs"], writes=["dq7"])
            S.op("act", f_act(Q(Q_T0), V(V_LAM), AF.Exp, scale=-1.0), reads=["vecs"], writes=["dq8"])
            S.op("dve", f_ts(Q(Q_T1), Q(Q_T0), -0.25, 1.0 / 3.0, ALU.mult, ALU.add), reads=["dq8"], writes=["dq9"])
            S.op("dve", f_tt(Q(Q_T1), Q(Q_T1), Q(Q_T0), ALU.mult), reads=["dq8", "dq9"], writes=["dq9"])
            S.op("dve", f_ts(Q(Q_T1), Q(Q_T1), -0.5, None, ALU.add), reads=["dq9"], writes=["dq9"])
            S.op("dve", f_tt(Q(Q_T1), Q(Q_T1), Q(Q_T0), ALU.mult), reads=["dq8", "dq9"], writes=["dq9"])
            S.op("dve", f_ts(Q(Q_T1), Q(Q_T1), 1.0, None, ALU.add), reads=["dq9"], writes=["dq9"])
            S.op("dve", f_tt(Q(Q_T1), Q(Q_T1), Q(Q_T0), ALU.mult), reads=["dq8", "dq9"], writes=["dq9"])
            S.op("dve", f_ts(Q(Q_NHSP), Q(Q_T1), -4.0, None, ALU.mult), reads=["dq9"], writes=["dq4"])
            S.op("dve", f_ts(Q(Q_QSP), Q(Q_T1), 2.0, None, ALU.mult), reads=["dq9"], writes=["dq5"])

            prefetch()

            def load_x(m, s):
                if m >= n_macro:
                    return
                t0 = m * TM + s * 128
                S.dma("sp", "xl%d" % (s % 2), xs[s % 2][:], x_d[t0:t0 + 128, :], writes=[("obuf", s % 2)])

            def load_h(m, s):
                t0 = m * TM + s * 128
                S.dma("sp", "hl0", hland[0][:], hscr_d[t0:t0 + 128, :],
                      reads=[("hscr", m, s)], writes=[("hA", 0)])


            def hTk(m, s, half):
                return ("hT", m % 3, s, half)

            def hT_all(m, kc):
                return [hTk(m, s, kc // 4) for s in range(NS)]

            def Athread(m, pool):
                hTm = hT[m % 3]
                load_x(m, 0)
                load_x(m, 1)
                for s in range(NS):
                    xk = ("obuf", s % 2)
                    xt = xs[s % 2]
                    hk = ("hA", 0)
                    hb = hA[0]
                    S.op("dve", lambda e, xt=xt: e.bn_stats(lst[:, 0:6], xt[:, 0:512]), reads=[xk], writes=["lst0"])
                    S.op("dve", lambda e, xt=xt: e.bn_stats(lst[:, 6:12], xt[:, 512:1024]), reads=[xk], writes=["lst1"])
                    S.op("dve", lambda e: e.bn_aggr(lmv[:], lst[:]), reads=["lst0", "lst1"], writes=["lmv"])
                    yield
                    S.op("pool", f_ts(lrs[:], lmv[:, 1:2], LN_EPS, None, ALU.add), reads=["lmv"], writes=["lrs"])
                    S.op("pool", f_tt(lrs[:], lrs[:], mhalf[:, 0:1], ALU.pow), reads=["lrs", "mhalf"], writes=["lrs"])
                    yield
                    S.op("dve", f_stt(hb[:], xt[:], lmv[:, 0:1], lnig[:], ALU.subtract, ALU.mult),
                         reads=[xk, "lmv", "lnig"], writes=[hk])
                    S.op("dve", f_stt(hb[:], hb[:], lrs[:], lnib[:], ALU.mult, ALU.add),
                         reads=[hk, "lrs", "lnib"], writes=[hk])
                    if s + 2 < NS:
                        load_x(m, s + 2)
                    t0 = m * TM + s * 128
                    S.dma("pool", "hs0", hscr_d[t0:t0 + 128, :], hb[:], reads=[hk], writes=[("hscr", m, s)])
                    yield
                    for half in range(2):
                        b = pool.get()
                        for j in range(4):
                            kc = half * 4 + j
                            S.op("pe", f_tr(pb[b][:, j * 128:(j + 1) * 128], hb[:, kc * 128:(kc + 1) * 128], identf[:]),
                                 reads=[hk, "identf"], writes=[PK(b)], inc=(j == 3))
                        yield
                        src = pb[b][:].rearrange("p (k t) -> p k t", k=4)
                        dst = hTm[:, half * 4:half * 4 + 4, s * 128:(s + 1) * 128]
                        S.op("act", f_acopy(dst, src), reads=[PK(b)], writes=[hTk(m, s, half)])
                    yield

            def Bthread(m, pool):
                hTm = hT[m % 3]
                S.dma("sp", "rotl", rot[:], rot_d[m], writes=["rot"])
                for u in range(4):
                    slot = use(m, u)
                    rk = ("ring", slot)
                    for s in range(NS):
                        b = pool.get()
                        for kc in range(8):
                            S.op("pe", f_mm(pb[b][:], hTm[:, kc, s * 128:(s + 1) * 128], tmw(slot, kc), kc == 0, kc == 7),
                                 reads=[hTk(m, s, kc // 4), rk], writes=[PK(b)], inc=(kc == 7))
                        if s == NS - 1:
                            done(m, u)
                        yield
                        if u < 2:
                            ps4 = pb[b][:].rearrange("p (h two d) -> p h two d", h=4, two=2)
                            rB4 = rB[:].rearrange("p (h two d) -> p h two d", h=4, two=2)
                            rA4 = rA[:].rearrange("p (h two d) -> p h two d", h=4, two=2)
                            cosb = rot[:, s * 128:s * 128 + 64].unsqueeze(1).unsqueeze(1).to_broadcast([128, 4, 2, 64])
                            sinb = rot[:, s * 128 + 64:s * 128 + 128].unsqueeze(1).to_broadcast([128, 4, 64])
                            S.op("dve", f_tt(rA4, ps4, cosb, ALU.mult), reads=[PK(b), "rot"], writes=["rA"])
                            S.op("dve", f_tt(rB4[:, :, 0, :], ps4[:, :, 1, :], sinb, ALU.mult), reads=[PK(b), "rot"], writes=["rB0"])
                            S.op("dve", f_tt(rB4[:, :, 1, :], ps4[:, :, 0, :], sinb, ALU.mult), reads=[PK(b), "rot"], writes=["rB1"])
                            yield
                            if u == 0:
                                q4 = qkr[:, s, 0:512].rearrange("p (h two d) -> p h two d", h=4, two=2)
                                S.op("pool", f_tt(q4[:, :, 0, :], rA4[:, :, 0, :], rB4[:, :, 0, :], ALU.subtract), reads=["rA", "rB0"], writes=[("qr", s)])
                                S.op("pool", f_tt(q4[:, :, 1, :], rA4[:, :, 1, :], rB4[:, :, 1, :], ALU.add), reads=["rA", "rB1"], writes=[("qr", s)])
                            else:
                                S.op("pool", f_tt(rA4[:, :, 0, :], rA4[:, :, 0, :], rB4[:, :, 0, :], ALU.subtract), reads=["rA", "rB0"], writes=["rA"])
                                S.op("pool", f_tt(rA4[:, :, 1, :], rA4[:, :, 1, :], rB4[:, :, 1, :], ALU.add), reads=["rA", "rB1"], writes=["rA"])
                                yield
                                S.op("act", f_acopy(qkr[:, s, 512:1024], rA[:]), reads=["rA"], writes=[("kr", s)])
                                decb = small[:, 0:4].unsqueeze(2).to_broadcast([128, 4, 128])
                                S.op("pool", f_tt(kdec[:, s, :].rearrange("p (h d) -> p h d", h=4),
                                                  rA[:].rearrange("p (h d) -> p h d", h=4), decb, ALU.mult),
                                     reads=["rA", "small"], writes=[("kdec", s)])
                        else:
                            vh = u - 2
                            S.op("act", f_acopy(vb[:, s, vh * 512:(vh + 1) * 512], pb[b][:]), reads=[PK(b)], writes=[("vb", s, vh)])
                        yield

            def Cthread(m, pool):
                for hp in range(2):
                    yield from Cpass(m, hp, pool)

            def Cpass(m, hp, pool):
                hTm = hT[m % 3]
                retTm = retT[m % 2]
                qT_, sT_, gs_, gm_, gr_, gn_, xh_ = qkT[0], sTm[0], gst[0], gmv[0], grs[0], gnm[0], xhat[0]
                K = lambda n: (n, 0)
                tgb, ynb, tgk, ynk = rA, rB, ["rA"], ["rB0", "rB1"]
                for s in range(NS):
                    bT = pool.get()
                    pT = pb[bT][:].bitcast(BF16)
                    for j in range(4):
                        hh = hp * 2 + (j % 2)
                        col = (0 if j < 2 else 512) + hh * 128
                        S.op("pe", f_tr(pT[:, j * 128:(j + 1) * 128], qkr[:, s, col:col + 128], identb[:]),
                             reads=[("qr", s) if j < 2 else ("kr", s), "identb"], writes=[PK(bT)], inc=(j == 3))
                    yield
                    S.op("act", f_acopy(qT_[:].rearrange("p a b -> p (a b)"), pT[:, 0:512]), reads=[PK(bT)], writes=[K("qkT")])
                    yield
                    bS = pool.get()
                    for hl in range(2):
                        S.op("pe", f_mm(pb[bS][:, hl * 128:(hl + 1) * 128], qT_[:, 2 + hl, :], qT_[:, hl, :], True, True),
                             reads=[K("qkT")], writes=[PK(bS)], inc=(hl == 1))
                    bK = pool.get()
                    for hl in range(2):
                        hh = hp * 2 + hl
                        S.op("pe", f_mm(pb[bK][:, hl * 256:(hl + 1) * 256], kdec[:, s, hh * 128:(hh + 1) * 128],
                                        vb[:, s, hh * 256:(hh + 1) * 256], True, True),
                             reads=[("kdec", s), ("vb", s, hp)], writes=[PK(bK)], inc=(hl == 1))
                    yield
                    S.op("dve", f_tt(sT_[:].rearrange("p a b -> p (a b)"), pb[bS][:, 0:256],
                                     maskT[:, hp * 256:(hp + 1) * 256], ALU.mult),
                         reads=[PK(bS), "maskT"], writes=[K("sTm")])
                    yield
                    bO = pool.get()
                    for hl in range(2):
                        hh = hp * 2 + hl
                        S.op("pe", f_mm(pb[bO][:, hl * 256:(hl + 1) * 256], sT_[:, hl, :], vb[:, s, hh * 256:(hh + 1) * 256], True, False),
                             reads=[K("sTm"), ("vb", s, hp)], writes=[PK(bO)], inc=False)
                        S.op("pe", f_mm(pb[bO][:, hl * 256:(hl + 1) * 256], qT_[:, hl, :], Sb[:, hh * 256:(hh + 1) * 256], False, True),
                             reads=[K("qkT"), ("Sb", hp)], writes=[PK(bO)], inc=(hl == 1))
                    yield
                    for hl in range(2):
                        hh = hp * 2 + hl
                        S.op("dve", f_stt(S32[:, hh * 256:(hh + 1) * 256], S32[:, hh * 256:(hh + 1) * 256], g128[hh],
                                          pb[bK][:, hl * 256:(hl + 1) * 256], ALU.mult, ALU.add),
                             reads=[PK(bK), ("S32", hh)], writes=[("S32", hh)])
                    yield
                    S.op("dve", f_copy(Sb[:, hp * 512:(hp + 1) * 512], S32[:, hp * 512:(hp + 1) * 512]),
                         reads=[("S32", hp * 2), ("S32", hp * 2 + 1)], writes=[("Sb", hp)])
                    for hl in range(2):
                        S.op("dve", lambda e, hl=hl, bO=bO: e.bn_stats(gs_[:, hl * 6:(hl + 1) * 6], pb[bO][:, hl * 256:(hl + 1) * 256]),
                             reads=[PK(bO)], writes=[K(("gst", hl))])
                        S.op("dve", lambda e, hl=hl: e.bn_aggr(gm_[:, hl, :], gs_[:, hl * 6:(hl + 1) * 6]),
                             reads=[K(("gst", hl))], writes=[K(("gmv", hl))])
                    yield
                    S.op("pool", f_tt(gr_[:], gm_[:, :, 1], small[:, 4 + hp * 2:6 + hp * 2], ALU.add),
                         reads=[K(("gmv", 0)), K(("gmv", 1)), "small"], writes=[K("grs")])
                    S.op("pool", f_tt(gr_[:], gr_[:], mhalf[:, 0:2], ALU.pow), reads=[K("grs"), "mhalf"], writes=[K("grs")])
                    S.op("pool", f_tt(gn_[:], gm_[:, :, 0], gr_[:], ALU.mult), reads=[K(("gmv", 0)), K(("gmv", 1)), K("grs")], writes=[K("gnm")])
                    S.op("pool", f_ts(gn_[:], gn_[:], -1.0, None, ALU.mult), reads=[K("gnm")], writes=[K("gnm")])
                    yield
                    for hl in range(2):
                        S.op("act", f_act(xh_[:, s, hl * 256:(hl + 1) * 256], pb[bO][:, hl * 256:(hl + 1) * 256], AF.Identity,
                                          bias=gn_[:, hl:hl + 1], scale=gr_[:, hl:hl + 1]),
                             reads=[PK(bO), K("grs"), K("gnm")], writes=[("xhat", 0, s, hl)])
                    yield
                for cl in range(4):
                    c = hp * 4 + cl
                    slot = use(m, 4 + hp)
                    rk = ("ring", slot)
                    bG = pool.get()
                    for kc in range(8):
                        S.op("pe", f_mm(pb[bG][:], slab(slot, cl, kc), hTm[:, kc, :], kc == 0, kc == 7),
                             reads=hT_all(m, kc) + [rk], writes=[PK(bG)], inc=(kc == 7))
                    if cl == 3:
                        done(m, 4 + hp)
                    bX = pool.get()
                    for s in range(NS):
                        S.op("pe", f_tr(pb[bX][:, s * 128:(s + 1) * 128], xh_[:, s, cl * 128:(cl + 1) * 128], identf[:]),
                             reads=[("xhat", 0, s, cl // 2), "identf"], writes=[PK(bX)], inc=(s == NS - 1))
                    yield
                    S.op("act", f_act(tgb[:], pb[bG][:], AF.Tanh, scale=0.5), reads=[PK(bG)], writes=tgk)
                    S.op("act", f_act(ynb[:], pb[bX][:], AF.Identity, bias=dq[:, Q_HGNB, c:c + 1], scale=dq[:, Q_HGNG, c:c + 1]),
                         reads=[PK(bX), "dq6", "dq7"], writes=ynk)
                    yield
                    S.op("dve", f_stt(tgb[:], tgb[:], 1.0, pb[bG][:], ALU.add, ALU.mult), reads=tgk + [PK(bG)], writes=tgk)
                    yield
                    S.op("pool", f_tt(retTm[:, c, :], ynb[:], tgb[:], ALU.mult), reads=ynk + tgk, writes=[("retT", m % 2, c)])
                    yield

            def Dthread(m, pool):
                hTm = hT[m % 3]
                ucount = {}

                def front(c):
                    u = 6 + c // 2
                    slot = use(m, u)
                    rk = ("ring", slot)
                    sl = (c % 2) * 2
                    di = c % 2
                    bXL = pool.get()
                    for kc in range(8):
                        S.op("pe", f_mm(pb[bXL][:], slab(slot, sl, kc), hTm[:, kc, :], kc == 0, kc == 7),
                             reads=hT_all(m, kc) + [rk], writes=[PK(bXL)], inc=(kc == 7))
                    xlk, xck, xcbk = ("xl", di), ("xc", di), ("xcb", di)
                    S.op("pool", f_copy(xl[di][:, 0:3], hist[:, c, :]), reads=[("hist", c)], writes=[xlk])
                    yield
                    S.op("act", f_acopy(xl[di][:, 3:515], pb[bXL][:]), reads=[PK(bXL)], writes=[xlk])
                    S.op("act", f_act(xc[di][:], pb[bXL][:], AF.Identity, bias=vecs[:, V_CB, c:c + 1], scale=vecs[:, V_CW3, c:c + 1]),
                         reads=[PK(bXL), "vecs"], writes=[xck])
                    bGL = pool.get()
                    for kc in range(8):
                        S.op("pe", f_mm(pb[bGL][:], slab(slot, sl + 1, kc), hTm[:, kc, :], kc == 0, kc == 7),
                             reads=hT_all(m, kc) + [rk], writes=[PK(bGL)], inc=(kc == 7))
                    ucount[u] = ucount.get(u, 0) + 1
                    if ucount[u] == 2:
                        done(m, u)
                    yield
                    S.op("pool", f_copy(hist[:, c, :], xl[di][:, 512:515]), reads=[xlk], writes=[("hist", c)])
                    for t, vi in ((2, V_CW2), (1, V_CW1), (0, V_CW0)):
                        S.op("dve", f_stt(xc[di][:], xl[di][:, t:t + 512], vecs[:, vi, c:c + 1], xc[di][:], ALU.mult, ALU.add),
                             reads=[xlk, xck, "vecs"], writes=[xck])
                    S.op("act", f_act(tgl[di][:], pb[bGL][:], AF.Tanh, scale=0.5), reads=[PK(bGL)], writes=[("tgl", di)])
                    yield
                    S.op("dve", f_copy(xcb[di][:], xc[di][:]), reads=[xck], writes=[xcbk])
                    S.op("dve", f_stt(tgl[di][:], tgl[di][:], 1.0, pb[bGL][:], ALU.add, ALU.mult), reads=[("tgl", di), PK(bGL)], writes=[("tgl", di)])
                    yield
                    bA = pool.get()
                    S.op("pe", f_mm(pb[bA][:], rgw[:, 0, c, :], xcb[di][:], True, True), reads=["rgw", xcbk], writes=[PK(bA)])
                    bI = pool.get()
                    S.op("pe", f_mm(pb[bI][:], rgw[:, 1, c, :], xcb[di][:], True, True), reads=["rgw", xcbk], writes=[PK(bI)])
                    S.op("act", f_act(ta[di][:], pb[bA][:], AF.Tanh, bias=dq[:, Q_HBA, c:c + 1], scale=0.5), reads=[PK(bA), "dq0"], writes=[("ta", di)])
                    S.op("act", f_act(ti[di][:], pb[bI][:], AF.Tanh, bias=dq[:, Q_HBX, c:c + 1], scale=0.5), reads=[PK(bI), "dq1"], writes=[("ti", di)])
                    yield
                    S.op("act", f_act(av[di][:], ta[di][:], AF.Exp, bias=dq[:, Q_NHSP, c:c + 1], scale=dq[:, Q_NHSP, c:c + 1]),
                         reads=[("ta", di), "dq4"], writes=[("av", di)])
                    S.op("act", f_act(ta[di][:], ta[di][:], AF.Tanh, bias=dq[:, Q_QSP, c:c + 1], scale=dq[:, Q_QSP, c:c + 1]),
                         reads=[("ta", di), "dq5"], writes=[("ta", di)])
                    S.op("dve", f_stt(ti[di][:], ti[di][:], 1.0, xc[di][:], ALU.add, ALU.mult), reads=[("ti", di), xck], writes=[("ti", di)])
                    yield

                def back(c):
                    di = c % 2
                    S.op("act", f_act(ta[di][:], ta[di][:], AF.Sqrt, scale=1.0 / 16.0), reads=[("ta", di)], writes=[("ta", di)])
                    yield
                    S.op("dve", f_stt(ta[di][:], av[di][:], 1.0, ta[di][:], ALU.add, ALU.mult), reads=[("av", di), ("ta", di)], writes=[("ta", di)])
                    yield
                    S.op("dve", f_tt(ta[di][:], ta[di][:], ti[di][:], ALU.mult), reads=[("ta", di), ("ti", di)], writes=[("ta", di)])
                    yield
                    S.op("dve", lambda e, c=c, di=di: e.tensor_tensor_scan(ti[di][:], av[di][:], ta[di][:], carry[:, c:c + 1], ALU.mult, ALU.add),
                         reads=[("av", di), ("ta", di), ("carry", c)], writes=[("ti", di)])
                    yield
                    S.op("pool", f_copy(carry[:, c:c + 1], ti[di][:, 511:512]), reads=[("ti", di)], writes=[("carry", c)])
                    S.op("pool", f_tt(lruT[m % 2][:, c, :], ti[di][:], tgl[di][:], ALU.mult), reads=[("ti", di), ("tgl", di)], writes=[("lruT", m % 2, c)])
                    yield

                for c0 in range(0, 8, 2):
                    yield from lockstep(front(c0), front(c0 + 1))
                    yield
                    yield
                    yield from lockstep(back(c0), back(c0 + 1))
                    yield

            def Ethread(m, pool):
                hTm = hT[m % 3]
                for c in range(8):
                    u = 10 + c
                    slot = use(m, u)
                    rk = ("ring", slot)
                    ei = 0
                    bMR = pool.get()
                    for kc in range(8):
                        S.op("pe", f_mm(pb[bMR][:], slab(slot, 0, kc), hTm[:, kc, :], kc == 0, kc == 7),
                             reads=hT_all(m, kc) + [rk], writes=[PK(bMR)], inc=(kc == 7))
                    yield
                    bML = pool.get()
                    for kc in range(8):
                        S.op("pe", f_mm(pb[bML][:], slab(slot, 1, kc), hTm[:, kc, :], kc == 0, kc == 7),
                             reads=hT_all(m, kc) + [rk], writes=[PK(bML)], inc=(kc == 7))
                    S.op("act", f_act(tr_[ei][:], pb[bMR][:], AF.Tanh, bias=dq[:, Q_HBM0, c:c + 1], scale=0.5), reads=[PK(bMR), "dq2"], writes=[("tr", ei)])
                    yield
                    bPR = pool.get()
                    for kc in range(8):
                        S.op("pe", f_mm(pb[bPR][:], slab(slot, 2, kc), retT[m % 2][:, kc, :], kc == 0, kc == 7),
                             reads=[("retT", m % 2, kc), rk], writes=[PK(bPR)], inc=(kc == 7))
                    S.op("act", f_act(tl_[ei][:], pb[bML][:], AF.Tanh, bias=dq[:, Q_HBM1, c:c + 1], scale=0.5), reads=[PK(bML), "dq3"], writes=[("tl", ei)])
                    yield
                    bPL = pool.get()
                    for kc in range(8):
                        S.op("pe", f_mm(pb[bPL][:], slab(slot, 3, kc), lruT[m % 2][:, kc, :], kc == 0, kc == 7),
                             reads=[("lruT", m % 2, kc), rk], writes=[PK(bPL)], inc=(kc == 7))
                    done(m, u)
                    S.op("dve", f_stt(tr_[ei][:], tr_[ei][:], 1.0, pb[bPR][:], ALU.add, ALU.mult), reads=[("tr", ei), PK(bPR)], writes=[("tr", ei)])
                    yield
                    S.op("dve", f_stt(tl_[ei][:], tl_[ei][:], 1.0, pb[bPL][:], ALU.add, ALU.mult), reads=[("tl", ei), PK(bPL)], writes=[("tl", ei)])
                    yield
                    S.op("pool", f_tt(mrg[:, c, :], tr_[ei][:], tl_[ei][:], ALU.add), reads=[("tr", ei), ("tl", ei)], writes=[("mrg", c)])
                    yield

            def Fthread(m, pool):
                load_h(m, 0)
                for s in range(NS):
                    ob = obuf[s % 2]
                    ok = ("obuf", s % 2)
                    hl = hland[0]
                    hlk = ("hA", 0)
                    bY = [pool.get(), pool.get()]
                    for hf in range(2):
                        slot = use(m, 18 + hf)
                        for kc in range(8):
                            S.op("pe", f_mm(pb[bY[hf]][:], mrg[:, kc, s * 128:(s + 1) * 128], tmw(slot, kc), kc == 0, kc == 7),
                                 reads=[("mrg", kc), ("ring", slot)], writes=[PK(bY[hf])], inc=(kc == 7))
                    if s == NS - 1:
                        done(m, 18)
                        done(m, 19)
                    yield
                    for hf in range(2):
                        S.op("dve", f_stt(ob[:, hf * 512:(hf + 1) * 512], hl[:, hf * 512:(hf + 1) * 512], 2.0 * ALPHA,
                                          pb[bY[hf]][:], ALU.mult, ALU.add),
                             reads=[hlk, PK(bY[hf])], writes=[ok])
                    if s + 1 < NS:
                        load_h(m, s + 1)
                    S.op("dve", lambda e, ob=ob: e.bn_stats(lst2[:, 0:6], ob[:, 0:512]), reads=[ok], writes=["lst20"])
                    S.op("dve", lambda e, ob=ob: e.bn_stats(lst2[:, 6:12], ob[:, 512:1024]), reads=[ok], writes=["lst21"])
                    S.op("dve", lambda e: e.bn_aggr(lmv2[:], lst2[:]), reads=["lst20", "lst21"], writes=["lmv2"])
                    yield
                    S.op("pool", f_ts(lrs2[:], lmv2[:, 1:2], 4.0 * LN_EPS, None, ALU.add), reads=["lmv2"], writes=["lrs2"])
                    S.op("pool", f_tt(lrs2[:], lrs2[:], mhalf[:, 0:1], ALU.pow), reads=["lrs2", "mhalf"], writes=["lrs2"])
                    yield
                    S.op("dve", f_stt(ob[:], ob[:], lmv2[:, 0:1], lnog[:], ALU.subtract, ALU.mult), reads=[ok, "lmv2", "lnog"], writes=[ok])
                    S.op("dve", f_stt(ob[:], ob[:], lrs2[:], lnob[:], ALU.mult, ALU.add), reads=[ok, "lrs2", "lnob"], writes=[ok])
                    t0 = m * TM + s * 128
                    S.dma("pool", "os%d" % (s % 2), out_d[t0:t0 + 128, :], ob[:], reads=[ok])
                    yield

            def chain(*gs):
                for g in gs:
                    yield from g

            run_gen(Athread(0, BankPool(S, [6, 7])))
            run_gen(Bthread(0, BankPool(S, [4, 5, 6, 7])))
            th = [(Cthread(0, BankPool(S, [0, 1])), 2), (Dthread(0, BankPool(S, [2, 3, 4])), 1)]
            if n_macro > 1:
                th.append((Athread(1, BankPool(S, [5])), 1))
            interleave(th)
            for m in range(n_macro):
                th = []
                if m + 1 < n_macro:
                    th.append((Bthread(m + 1, BankPool(S, [4, 5, 6, 7])), 2))
                if m >= 1:
                    th.append((Fthread(m - 1, BankPool(S, [0, 1, 2, 3])), 1))
                interleave(th)
                pEA = BankPool(S, [5, 6, 7])
                th = [(Ethread(m, pEA), 1)]
                if m + 1 < n_macro:
                    th.append((Cthread(m + 1, BankPool(S, [0, 1])), 2))
                    th.append((Dthread(m + 1, BankPool(S, [2, 3, 4])), 1))
                if m + 2 < n_macro:
                    th.append((Athread(m + 2, pEA), 1))
                interleave(th)
            run_gen(Fthread(n_macro - 1, BankPool(S, [0, 1, 2, 3])))
            S.wait_all("pool", [("obuf", 0), ("obuf", 1)] + [("hscr", n_macro - 1, s) for s in range(NS)])
            S.wait_all("sp", [("ring", i) for i in range(RING)] + [("hA", 0), "rot"])
            S.wait_all("act", [("ring", i) for i in range(RING)] + [("wscr", u) for u in range(NU)])

        record = []
        emit_all(Sched(nc, None, dry=True), None, record)
        S = Sched(nc, st)
        emit_all(S, record, None)
        with nc.Block() as block:
            S.emit(block)
    return nc


def _tm_unit(W, col0):
    blk = W[:, col0:col0 + 512].reshape(8, 128, 512).transpose(1, 0, 2)
    return np.ascontiguousarray(blk).reshape(128, 4096)


def _slab_unit(slabs):
    out = np.empty((128, 4, 8, 128), np.float32)
    for i, (W, col0) in enumerate(slabs):
        out[:, i] = W[:, col0:col0 + 128].reshape(8, 128, 128).transpose(1, 0, 2)
    return out.reshape(128, 4096)


def _fm(v):
    return np.ascontiguousarray(np.asarray(v, np.float32).reshape(8, 128).T)


def _const_tables():
    half = 64
    inv = (np.float32(10000.0) ** (np.float32(-2.0) * np.arange(half, dtype=np.float32) / np.float32(128))).astype(np.float32)
    ang = (np.arange(SEQ, dtype=np.float32)[:, None] * inv[None, :]).astype(np.float32)
    cos = np.cos(ang.astype(np.float64)).astype(np.float32)
    sin = np.sin(ang.astype(np.float64)).astype(np.float32)
    tab = np.concatenate([cos, sin], axis=1)
    rot = tab.reshape(NM, NS, 128, 128).transpose(0, 2, 1, 3).reshape(NM, 128, NS * 128)
    rot = np.ascontiguousarray(rot)
    lg = np.log1p(-np.exp2(-5.0 - np.arange(4, dtype=np.float64)))
    i = np.arange(128, dtype=np.float64)[:, None]
    j = np.arange(128, dtype=np.float64)[None, :]
    ci = (i // 64)
    cj = (j // 64)
    maskT = np.zeros((128, 4, 128), np.float64)
    for h in range(4):
        e = np.where(ci == cj, np.abs(i - j), i - j) - (i + 1.0)
        mk = np.where(cj > ci, 0.0, np.exp(lg[h] * e)) * (128.0 ** -0.5)
        maskT[:, h, :] = mk.T
    small = np.zeros((128, 8), np.float64)
    jj = np.arange(128, dtype=np.float64)
    for h in range(4):
        small[:, h] = (128.0 ** -0.5) * np.exp(lg[h] * (127.0 - jj))
        small[:, 4 + h] = LN_EPS / np.exp(2.0 * lg[h] * (jj + 1.0))
    return rot, maskT.reshape(128, 512).astype(np.float32), small.astype(np.float32)


def _prep_shared(inp):
    W = np.asarray(inp["w_in"], np.float32)[0]
    Pr = np.asarray(inp["w_ret_proj"], np.float32)[0]
    Pl = np.asarray(inp["w_lru_proj"], np.float32)[0]
    Wo = np.asarray(inp["w_out"], np.float32)[0]
    units = []
    units.append(_tm_unit(W, 0))
    units.append(_tm_unit(W, 512))
    units.append(_tm_unit(W, 1024))
    units.append(_tm_unit(W, 1536))
    for hp in range(2):
        units.append(_slab_unit([(W, 2048 + (hp * 4 + cl) * 128) for cl in range(4)]))
    for jd in range(4):
        c0, c1 = 2 * jd, 2 * jd + 1
        units.append(_slab_unit([(W, 3072 + c0 * 128), (W, 4096 + c0 * 128), (W, 3072 + c1 * 128), (W, 4096 + c1 * 128)]))
    for c in range(8):
        units.append(_slab_unit([(W, 5120 + c * 128), (W, 6144 + c * 128), (Pr, c * 128), (Pl, c * 128)]))
    units.append(_tm_unit(Wo, 0))
    units.append(_tm_unit(Wo, 512))
    wpack = np.stack(units, 0)
    wa = np.asarray(inp["w_rg_a"], np.float32)[0]
    wx = np.asarray(inp["w_rg_x"], np.float32)[0]
    rgw = np.stack([wa.transpose(1, 0, 2), wx.transpose(1, 0, 2)], axis=1)
    rgw = np.ascontiguousarray(rgw).reshape(128, 2048)
    cw = np.asarray(inp["conv_w"], np.float32)[0]
    bm = np.asarray(inp["b_merge"], np.float32)[0]
    vl = [cw[0], cw[1], cw[2], cw[3], inp["conv_b"][0], inp["b_rg_a"][0], inp["b_rg_x"][0], inp["lru_lambda"][0],
          bm[0], bm[1], inp["ret_gn_g"][0], inp["ret_gn_b"][0]]
    vecs = np.stack([_fm(v) for v in vl], axis=1).reshape(128, NV * 8)
    lnv = np.stack([np.asarray(inp["ln_in_g"], np.float32), np.asarray(inp["ln_in_b"], np.float32),
                    np.asarray(inp["ln_out_g"], np.float32)[0], np.asarray(inp["ln_out_b"], np.float32)[0]], 0)
    rot, maskT, small = _const_tables()
    return {"wpack": wpack, "rgw": rgw, "vecs": np.ascontiguousarray(vecs), "lnv": np.ascontiguousarray(lnv),
            "rot": rot, "maskT": maskT, "small": small, "ident": np.eye(128, dtype=np.float32)}


def kernel(**inputs):
    x = np.asarray(inputs["x"], np.float32)
    shared = _prep_shared(inputs)
    nc = build_program()
    in_maps = []
    for b in range(NCORES):
        d = dict(shared)
        d["x"] = np.ascontiguousarray(x[b])
        in_maps.append(d)
    res = run_bass_kernel_spmd(nc, in_maps, core_ids=list(range(NCORES)))
    out = np.stack([np.asarray(res.results[b]["out"], np.float32) for b in range(NCORES)], 0)
    return out
```
